# Optimizing a Trainium2 kernel written in Bass

```python
import math
import jax, jax.numpy as jnp
from jax import lax
import numpy as np

D_MODEL = 1024
BATCH = 4
SEQ = 4096
DEPTH = 4
DEC_BATCH = 128
DEC_SEQ = 1
PAST_LEN = 8192
PAGE_SIZE = 128

N_MIXERS = 2
N_RET = (DEPTH + 1) // 2
N_SWA = DEPTH // 2

RET_HEADS = 4
RET_DK = 256
RET_DV = 512
RET_QK = RET_HEADS * RET_DK
RET_V = RET_HEADS * RET_DV
RET_IN = 2 * RET_QK + 2 * RET_V
RET_CHUNK = 128

SWA_HEADS = 16
SWA_KV_HEADS = 2
SWA_GROUP = SWA_HEADS // SWA_KV_HEADS
SWA_HD = 64
SWA_Q = SWA_HEADS * SWA_HD
SWA_KV = SWA_KV_HEADS * SWA_HD
SWA_IN = 2 * SWA_Q + 2 * SWA_KV
WINDOW = 128
SWA_BLOCK = WINDOW

EPS = 1e-6
NEG = -1e30

kernel_name = "hybrid_retention_swa_decode_step"


def rmsnorm(x, g):
    xf = x.astype(jnp.float32)
    y = xf * lax.rsqrt(jnp.mean(xf * xf, axis=-1, keepdims=True) + EPS)
    return (y * g.astype(jnp.float32)).astype(x.dtype)


def retention_scan(q, k, v, S0):
    B, L = q.shape[0], q.shape[1]
    C = RET_CHUNK if L % RET_CHUNK == 0 else L
    n = L // C
    log_g = jnp.log(1.0 - jnp.exp2(-5.0 - jnp.arange(RET_HEADS, dtype=jnp.float32)))
    idx = jnp.arange(C, dtype=jnp.float32)
    dist = idx[:, None] - idx[None, :]
    dmask = jnp.where(dist[None] >= 0, jnp.exp(log_g[:, None, None] * jnp.maximum(dist, 0.0)[None]), 0.0)
    cross_decay = jnp.exp(log_g[None, :] * (idx[:, None] + 1.0))
    k_decay = jnp.exp(log_g[None, :] * (C - 1.0 - idx[:, None]))
    chunk_decay = jnp.exp(log_g * C)

    def to_chunks(a):
        a = a.astype(jnp.float32).reshape(B, n, C, a.shape[2], a.shape[3])
        return jnp.moveaxis(a, 1, 0)

    def step(S, inp):
        qc, kc, vc = inp
        inner = jnp.einsum('bthd,bshd->bhts', qc, kc) * dmask[None]
        o = jnp.einsum('bhts,bshe->bthe', inner, vc) \
            + jnp.einsum('bthd,bhde->bthe', qc, S) * cross_decay[None, :, :, None]
        S_new = S * chunk_decay[None, :, None, None] \
            + jnp.einsum('bshd,bshe->bhde', kc * k_decay[None, :, :, None], vc)
        return S_new, o

    S_fin, o = lax.scan(step, S0.astype(jnp.float32), (to_chunks(q), to_chunks(k), to_chunks(v)))
    o = jnp.moveaxis(o, 0, 1).reshape(B, L, RET_HEADS, RET_DV)
    return o.astype(v.dtype), S_fin


def retention_layer(x, S0, g_norm, w_in, g_out, w_out):
    B, L, _ = x.shape
    h = rmsnorm(x, g_norm)
    proj = h @ w_in
    q, k, v, gate = jnp.split(proj, [RET_QK, 2 * RET_QK, 2 * RET_QK + RET_V], axis=-1)
    q = q.reshape(B, L, RET_HEADS, RET_DK)
    k = k.reshape(B, L, RET_HEADS, RET_DK) * (RET_DK ** -0.5)
    v = v.reshape(B, L, RET_HEADS, RET_DV)
    o, S = retention_scan(q, k, v, S0)
    o = rmsnorm(o, g_out.reshape(RET_HEADS, RET_DV)).reshape(B, L, RET_V)
    o = o * jax.nn.silu(gate)
    return x + o @ w_out, S.astype(S0.dtype)


def alibi_slopes():
    h = jnp.arange(1, SWA_HEADS + 1, dtype=jnp.float32)
    return jnp.exp2(-8.0 * h / SWA_HEADS).reshape(SWA_KV_HEADS, SWA_GROUP)


def sink_softmax(scores, sinks):
    sink_col = jnp.broadcast_to(sinks.astype(jnp.float32)[:, :, None, None], scores.shape[:-1] + (1,))
    p = jax.nn.softmax(jnp.concatenate([scores, sink_col], axis=-1), axis=-1)
    return p[..., :-1]


def swa_attend_prompt(q, k, v, sinks):
    B, L = q.shape[0], q.shape[1]
    nb = L // SWA_BLOCK
    qb = q.reshape(B, nb, SWA_BLOCK, SWA_KV_HEADS, SWA_GROUP, SWA_HD)

    def band(a):
        pad = jnp.zeros((B, SWA_BLOCK) + a.shape[2:], a.dtype)
        prev = jnp.concatenate([pad, a[:, :L - SWA_BLOCK]], axis=1).reshape(B, nb, SWA_BLOCK, SWA_KV_HEADS, SWA_HD)
        cur = a.reshape(B, nb, SWA_BLOCK, SWA_KV_HEADS, SWA_HD)
        return jnp.concatenate([prev, cur], axis=2)

    kk, vv = band(k), band(v)
    scores = jnp.einsum('bnqkgd,bnskd->bnkgqs', qb, kk).astype(jnp.float32) * (SWA_HD ** -0.5)
    j = jnp.arange(SWA_BLOCK)[:, None]
    u = jnp.arange(2 * SWA_BLOCK)[None, :]
    dist = SWA_BLOCK + j - u
    in_band = (dist >= 0) & (dist <= WINDOW)
    valid = in_band[None] & ((jnp.arange(nb)[:, None, None] > 0) | (u[None] >= SWA_BLOCK))
    scores = scores - alibi_slopes()[:, :, None, None] * dist.astype(jnp.float32)
    scores = jnp.where(valid[None, :, None, None], scores, NEG)
    p = sink_softmax(scores, sinks).astype(v.dtype)
    o = jnp.einsum('bnkgqs,bnskd->bnqkgd', p, vv)
    return o.reshape(B, L, SWA_Q)


def swa_attend_sample(q, k, v, k_buf, v_buf, sinks):
    B, T = q.shape[0], q.shape[1]
    Wb = k_buf.shape[1]
    kk = jnp.concatenate([k_buf, k], axis=1)
    vv = jnp.concatenate([v_buf, v], axis=1)
    scores = jnp.einsum('btkgd,bskd->bkgts', q, kk).astype(jnp.float32) * (SWA_HD ** -0.5)
    dist = Wb + jnp.arange(T)[:, None] - jnp.arange(Wb + T)[None, :]
    valid = (dist >= 0) & (dist <= WINDOW)
    scores = scores - alibi_slopes()[:, :, None, None] * dist.astype(jnp.float32)
    scores = jnp.where(valid, scores, NEG)
    p = sink_softmax(scores, sinks).astype(v.dtype)
    o = jnp.einsum('bkgts,bskd->btkgd', p, vv).reshape(B, T, SWA_Q)
    return o, kk[:, -Wb:], vv[:, -Wb:]


def swa_layer(x, g_norm, w_in, q_gain, k_gain, sinks, w_out, k_buf=None, v_buf=None):
    B, L, _ = x.shape
    h = rmsnorm(x, g_norm)
    proj = h @ w_in
    q, k, v, gate = jnp.split(proj, [SWA_Q, SWA_Q + SWA_KV, SWA_Q + 2 * SWA_KV], axis=-1)
    q = rmsnorm(q.reshape(B, L, SWA_KV_HEADS, SWA_GROUP, SWA_HD), q_gain)
    k = rmsnorm(k.reshape(B, L, SWA_KV_HEADS, SWA_HD), k_gain)
    v = v.reshape(B, L, SWA_KV_HEADS, SWA_HD)
    if k_buf is None:
        o = swa_attend_prompt(q, k, v, sinks)
        w = min(WINDOW, L)
        new_k, new_v = k[:, L - w:], v[:, L - w:]
    else:
        o, new_k, new_v = swa_attend_sample(q, k, v, k_buf, v_buf, sinks)
    o = o * jax.nn.silu(gate)
    return x + o @ w_out, new_k, new_v


def setup_inputs(seed: int = 0) -> dict:
    key = jax.random.key(seed)
    ks = jax.random.split(key, 16)
    f32 = jnp.float32
    wb = min(WINDOW, PAST_LEN)
    nrm = lambda k, s: jax.random.normal(k, s, f32)
    return {
        "x_prompt": nrm(ks[0], (BATCH, SEQ, D_MODEL)),
        "x_sample": nrm(ks[1], (DEC_BATCH, DEC_SEQ, D_MODEL)),
        "state_ret": nrm(ks[2], (N_RET, DEC_BATCH, RET_HEADS, RET_DK, RET_DV)) * 0.1,
        "cache_swa_k": nrm(ks[3], (N_SWA, DEC_BATCH, wb, SWA_KV_HEADS, SWA_HD)),
        "cache_swa_v": nrm(ks[4], (N_SWA, DEC_BATCH, wb, SWA_KV_HEADS, SWA_HD)),
        "norm_ret": 1.0 + 0.02 * nrm(ks[5], (N_RET, D_MODEL)),
        "w_in_ret": nrm(ks[6], (N_RET, D_MODEL, RET_IN)) * D_MODEL ** -0.5,
        "ret_out_norm": 1.0 + 0.02 * nrm(ks[7], (N_RET, RET_V)),
        "w_out_ret": nrm(ks[8], (N_RET, RET_V, D_MODEL)) * RET_V ** -0.5,
        "norm_swa": 1.0 + 0.02 * nrm(ks[9], (N_SWA, D_MODEL)),
        "w_in_swa": nrm(ks[10], (N_SWA, D_MODEL, SWA_IN)) * D_MODEL ** -0.5,
        "q_norm": 1.0 + 0.02 * nrm(ks[11], (N_SWA, SWA_HD)),
        "k_norm": 1.0 + 0.02 * nrm(ks[12], (N_SWA, SWA_HD)),
        "sinks": 0.5 * nrm(ks[13], (N_SWA, SWA_KV_HEADS, SWA_GROUP)),
        "w_out_swa": nrm(ks[14], (N_SWA, SWA_Q, D_MODEL)) * SWA_Q ** -0.5,
    }


def reference(x_prompt, x_sample, state_ret, cache_swa_k, cache_swa_v,
              norm_ret, w_in_ret, ret_out_norm, w_out_ret,
              norm_swa, w_in_swa, q_norm, k_norm, sinks, w_out_swa):
    xp, xs = x_prompt, x_sample
    ret_p, ret_s, kp_l, vp_l, ks_l, vs_l = [], [], [], [], [], []
    for i in range(DEPTH):
        li = i // N_MIXERS
        if i % N_MIXERS == 0:
            S0p = jnp.zeros((xp.shape[0], RET_HEADS, RET_DK, RET_DV), xp.dtype)
            xp, Sp = retention_layer(xp, S0p, norm_ret[li], w_in_ret[li], ret_out_norm[li], w_out_ret[li])
            xs, Ss = retention_layer(xs, state_ret[li], norm_ret[li], w_in_ret[li], ret_out_norm[li], w_out_ret[li])
            ret_p.append(Sp)
            ret_s.append(Ss)
        else:
            xp, kp, vp = swa_layer(xp, norm_swa[li], w_in_swa[li], q_norm[li], k_norm[li], sinks[li], w_out_swa[li])
            xs, kS, vS = swa_layer(xs, norm_swa[li], w_in_swa[li], q_norm[li], k_norm[li], sinks[li], w_out_swa[li],
                                   cache_swa_k[li], cache_swa_v[li])
            kp_l.append(kp); vp_l.append(vp); ks_l.append(kS); vs_l.append(vS)
    return (xp, xs, jnp.stack(ret_p), jnp.stack(ret_s), jnp.stack(kp_l), jnp.stack(vp_l), jnp.stack(ks_l), jnp.stack(vs_l))
```

```python
import math
from contextlib import ExitStack

import numpy as np
import concourse.bass as bass
import concourse.mybir as mybir
from concourse.bass_utils import run_bass_kernel_spmd

F32 = mybir.dt.float32
BF16 = mybir.dt.bfloat16
ALU = mybir.AluOpType
AF = mybir.ActivationFunctionType
AX = mybir.AxisListType

D = 1024
RH, RDK, RDV = 4, 256, 512
RQK, RV = 1024, 2048
RIN = 6144
SH, SKV, SG, SHD = 16, 2, 8, 64
SQ, SKVW = 1024, 128
SIN = 2304
EPS = 1e-6
NS = 16
GAM = [1.0 - 2.0 ** (-5.0 - h) for h in range(RH)]
SLOPE = [2.0 ** (-8.0 * (h + 1) / SH) for h in range(SH)]


class DSem:
    __slots__ = ("name", "dsem", "dcnt")

    def __init__(self, name):
        self.name = name
        self.dsem = None
        self.dcnt = 0


class Res:
    __slots__ = ("name", "w", "r", "excl")

    def __init__(self, name):
        self.name = name
        self.w = None
        self.r = []
        self.excl = False


class T:
    __slots__ = ("ap", "res")

    def __init__(self, ap, name):
        self.ap = ap
        self.res = Res(name)


class Sched:
    ENG = ("pe", "act", "dve", "pool", "sp")

    def __init__(self):
        self.prog = {e: [] for e in self.ENG}
        self.cnt = {e: 0 for e in self.ENG}
        self.waited = {e: {} for e in self.ENG}
        self.dres = []
        self.dsems = {}

    def _deps(self, reads, writes):
        t = []
        for r in reads:
            if r.w is not None:
                t.append(r.w)
            if r.excl:
                t.extend(r.r)
        for w in writes:
            if w.w is not None:
                t.append(w.w)
            t.extend(w.r)
        return t

    def _upd(self, reads, writes, tok):
        for r in reads:
            r.r = [x for x in r.r if x[0] != tok[0]] + [tok]
        for w in writes:
            w.w = tok
            w.r = []

    def _need(self, eng, toks):
        for key, val in toks:
            if key == "pe" and eng == "pe":
                continue
            if isinstance(key, DSem):
                val = key.dcnt
            if self.waited[eng].get(key, 0) >= val:
                continue
            self.waited[eng][key] = val
            self.prog[eng].append(("wait", key, val))

    def op(self, eng, fn, reads=(), writes=()):
        reads = [x.res if isinstance(x, T) else x for x in reads]
        writes = [x.res if isinstance(x, T) else x for x in writes]
        self._need(eng, self._deps(reads, writes))
        self.cnt[eng] += 1
        self.prog[eng].append(("op", fn))
        self._upd(reads, writes, (eng, self.cnt[eng]))

    def dma(self, eng, out, in_, reads=(), writes=(), res=None, grp=None, **kw):
        reads = [x.res if isinstance(x, T) else x for x in reads]
        writes = [x.res if isinstance(x, T) else x for x in writes]
        kind = "sw" if eng == "pool" else "hw"
        if grp is None:
            grp = (writes + reads)[0].name if (writes or reads) else "misc" + str(len(self.dsems))
        key = (grp, kind)
        if key not in self.dsems:
            self.dsems[key] = DSem(f"{grp}_{kind}")
            self.dres.append(self.dsems[key])
        res = self.dsems[key]
        self._need(eng, self._deps(reads, writes))
        res.dcnt += 16
        self.prog[eng].append(("dma", out, in_, res, kw))
        self._upd(reads, writes, (res, res.dcnt))

    def barrier(self):
        for e in self.ENG:
            toks = [(o, self.cnt[o]) for o in self.ENG if o != e and o != "sp" and self.cnt[o] > 0]
            toks += [(r, r.dcnt) for r in self.dres]
            if e != "sp" and e != "pe" and self.cnt[e] > 0:
                toks.append((e, self.cnt[e]))
            self._need(e, toks)


def _consts():
    c = {}
    c["ident"] = np.eye(128, dtype=np.float32)
    t = np.arange(128, dtype=np.float64)
    cq = np.zeros((128, 8, 128), np.float64)
    ck = np.zeros((128, 8, 128), np.float64)
    kd = np.zeros((128, 4), np.float64)
    for h in range(RH):
        lg = math.log(GAM[h])
        for dc in range(2):
            cq[:, h * 2 + dc, :] = np.exp(lg * (t + 1.0))[None, :]
            ck[:, h * 2 + dc, :] = (np.exp(-lg * (t + 1.0)) * RDK ** -0.5)[None, :]
        kd[:, h] = np.exp(lg * (127.0 - t)) * RDK ** -0.5
    c["cq"] = cq.reshape(128, 1024).astype(np.float32)
    c["ck"] = ck.reshape(128, 1024).astype(np.float32)
    c["kd"] = kd.astype(np.float32)
    s = np.arange(128)[:, None]
    q = np.arange(128)[None, :]
    c["causal"] = (q >= s).astype(np.float32)
    E = np.zeros((128, SH, 2, 128), np.float64)
    for h in range(SH):
        dprev = 128 + q - s
        E[:, h, 0, :] = np.where(s >= q, np.exp(-SLOPE[h] * dprev), 0.0)
        dcur = q - s
        E[:, h, 1, :] = np.where(s <= q, np.exp(-SLOPE[h] * np.maximum(dcur, 0)), 0.0)
    c["E"] = E.reshape(128, SH * 2 * 128).astype(np.float32)
    al = np.zeros((128, SH), np.float64)
    for h in range(SH):
        al[:, h] = -SLOPE[h] * (128.0 - np.arange(128))
    c["alibi"] = al.astype(np.float32)
    oh = np.zeros((128, NS, NS), np.float32)
    for b in range(NS):
        oh[:, b, b] = 1.0
    c["onehot"] = oh.reshape(128, NS * NS)
    c["eps"] = np.full((128, 1), EPS, np.float32)
    return c


CORDER = ["ident", "kd", "causal", "alibi", "onehot", "eps", "cq", "ck", "E"]
CSMALL = ["ident", "kd", "causal", "alibi", "onehot", "eps"]


def _cst_layout():
    c = _consts()
    off = {}
    o = 0
    for k in CORDER:
        off[k] = (o, c[k].shape[1])
        o += c[k].shape[1]
    return c, off, o


class _Stop(Exception):
    pass


STOP = [0]


def build_nc(NCH, DEPTH=4):
    TT = NCH * 128
    NRET = (DEPTH + 1) // 2
    NSWA = DEPTH // 2
    nc = bass.Bass("TRN2", target_bir_lowering=False)
    S = Sched()
    _, coff, CW = _cst_layout()
    CWS = sum(coff[k][1] for k in CSMALL)

    def din(name, shape):
        return nc.dram_tensor(name, list(shape), F32, kind="ExternalInput").ap()

    def dout(name, shape):
        return nc.dram_tensor(name, list(shape), F32, kind="ExternalOutput").ap()

    xp = din("xp", [TT, D])
    xs = din("xs", [NS, D])
    st = din("st", [max(NRET, 1), NS, RH, RDK, RDV])
    ckd = din("ck", [max(NSWA, 1), NS, 128, 128])
    cvd = din("cv", [max(NSWA, 1), NS, 128, 128])
    norm_ret = din("norm_ret", [2, D])
    w_in_ret = din("w_in_ret", [2, D, RIN])
    ret_out_norm = din("ret_out_norm", [2, RV])
    w_out_ret = din("w_out_ret", [2, RV, D])
    norm_swa = din("norm_swa", [2, D])
    w_in_swa = din("w_in_swa", [2, D, SIN])
    q_norm = din("q_norm", [2, SHD])
    k_norm = din("k_norm", [2, SHD])
    sinks = din("sinks", [2, SH])
    w_out_swa = din("w_out_swa", [2, SQ, D])
    cst = din("cst", [128, CW])

    yp = dout("yp", [TT, D])
    ys = dout("ys", [NS, D])
    rp = dout("rp", [max(NRET, 1), RH, RDK, RDV])
    rs = dout("rs", [max(NRET, 1), NS, RH, RDK, RDV])
    kp = dout("kp", [max(NSWA, 1), 128, 128])
    vp = dout("vp", [max(NSWA, 1), 128, 128])
    ks = dout("ks", [max(NSWA, 1), NS, 128, 128])
    vs = dout("vs", [max(NSWA, 1), NS, 128, 128])
    xa = nc.dram_tensor("xa", [TT, D], F32, kind="Internal").ap()
    xbd = nc.dram_tensor("xbd", [TT, D], F32, kind="Internal").ap()

    TOTAL = 106400
    es = ExitStack()
    big = es.enter_context(nc.sbuf_tensor("big", [128, TOTAL], BF16))
    bigap = big[:]
    banks = []
    for i in range(8):
        pt = es.enter_context(nc.psum_tensor(f"ps{i}", [128, 512], F32))
        banks.append(T(pt[:], f"ps{i}"))
        banks[-1].res.excl = True

    class Alloc:
        def __init__(self):
            self.off = 0
            self.n = 0
            self.rescache = {}

        def __call__(self, shape, dtype=BF16, parts=128, name=None):
            n = 1
            for s_ in shape:
                n *= s_
            ne = n * (2 if dtype == F32 else 1)
            self.off = (self.off + 15) // 16 * 16
            assert self.off + ne <= TOTAL, ("SBUF overflow", name, self.off, ne)
            ap = bigap[0:parts, self.off:self.off + ne]
            self.off += ne
            if dtype == F32:
                ap = ap.bitcast(F32)
            if len(shape) == 2:
                ap = ap.rearrange("p (a b) -> p a b", a=shape[0])
            elif len(shape) == 3:
                ap = ap.rearrange("p (a b c) -> p a b c", a=shape[0], b=shape[1])
            self.n += 1
            t_ = T(ap, name or f"t{self.n}")
            if name is not None:
                if name in self.rescache:
                    t_.res = self.rescache[name]
                else:
                    self.rescache[name] = t_.res
            return t_

    A = Alloc()

    CST = A([CWS], F32, name="cst")

    def cs(k):
        o, w = coff[k]
        return CST.ap[:, o:o + w]

    ident = A([128], BF16, name="ident")
    onehotB = A([NS, NS], BF16, name="onehotB")
    Wbig = A([8, RIN], BF16, name="Wbig")
    Wout = A([16, D], BF16, name="Wout")
    gB = A([D], F32, name="gB")
    hb = A([D], BF16, name="hb")
    hT = A([8, 128], BF16, name="hT")
    junk = T(hb.ap, "junk")
    junk.res = hb.res
    stat = A([64], F32, name="stat")
    PH0 = A.off
    xsd = nc.dram_tensor("xsd", [NS, D], F32, kind="Internal").ap()

    bank_i = [0]
    pinned = set()

    def nb():
        while True:
            b = banks[bank_i[0] % 8]
            i = bank_i[0] % 8
            bank_i[0] += 1
            if i not in pinned:
                return b

    def bf(bank):
        return bank.ap.bitcast(BF16)

    def act(out, in_, func, reads, writes, **kw):
        S.op("act", lambda e: e.activation(out=out, in_=in_, func=func, **kw), reads, writes)

    def tt(eng, out, in0, in1, op, reads, writes):
        S.op(eng, lambda e: e.tensor_tensor(out=out, in0=in0, in1=in1, op=op), reads, writes)

    def ts(eng, out, in0, s1, s2, op0, op1, reads, writes):
        if op1 is None:
            S.op(eng, lambda e: e.tensor_scalar(out=out, in0=in0, scalar1=s1, scalar2=None, op0=op0), reads, writes)
        else:
            S.op(eng, lambda e: e.tensor_scalar(out=out, in0=in0, scalar1=s1, scalar2=s2, op0=op0, op1=op1), reads, writes)

    def stt(eng, out, in0, scalar, in1, op0, op1, reads, writes):
        S.op(eng, lambda e: e.scalar_tensor_tensor(out=out, in0=in0, scalar=scalar, in1=in1, op0=op0, op1=op1), reads, writes)

    def cp(eng, out, in_, reads, writes):
        if eng == "act":
            S.op("act", lambda e: e.copy(out=out, in_=in_), reads, writes)
        else:
            S.op(eng, lambda e: e.tensor_copy(out=out, in_=in_), reads, writes)

    def mm(out, lhsT, rhs, start, stop, reads, writes):
        S.op("pe", lambda e: e.matmul(out, lhsT=lhsT, rhs=rhs, start=start, stop=stop), reads, writes)

    def tr(out, in_, idn, reads, writes):
        S.op("pe", lambda e: e.transpose(out=out, in_=in_, identity=idn), reads, writes)

    def memset(eng, t, val):
        S.op(eng, lambda e: e.memset(t.ap, val), [], [t])

    def rsqrt_stat(dst, src, n, scale, P, reads, writes_t):
        act(dst[0:P, 0:n], src[0:P, 0:n], AF.Ln, reads + [CST], [writes_t],
            bias=cs("eps")[0:P, :], scale=scale)
        act(dst[0:P, 0:n], dst[0:P, 0:n], AF.Exp, [writes_t], [writes_t], scale=-0.5)

    def stat_t(lo, n, name):
        return T(stat.ap[:, lo:lo + n], name)

    ssq = stat_t(0, 1, "ssq")
    rstd = stat_t(1, 1, "rstd")
    ssqo = stat_t(2, 4, "ssqo")
    rstdo = stat_t(6, 4, "rstdo")
    ssq18 = stat_t(10, 18, "ssq18")
    rq18 = stat_t(28, 18, "rq18")
    den = stat_t(46, 16, "den")

    def rmsnorm_to_hT(xt, P, ):
        act(junk.ap[0:P, :], xt.ap[0:P, :], AF.Square, [xt], [junk, ssq], accum_out=ssq.ap[0:P, :])
        rsqrt_stat(rstd.ap, ssq.ap, 1, 1.0 / D, P, [ssq], rstd)
        stt("dve", hb.ap[0:P, :], xt.ap[0:P, :], rstd.ap[0:P, :], gB.ap[0:P, :], ALU.mult, ALU.mult,
            [xt, rstd, gB], [hb])
        bk = nb()
        for kc in range(8):
            tr(bf(bk)[:, kc * P:(kc + 1) * P], hb.ap[0:P, kc * 128:(kc + 1) * 128], ident.ap[0:P, 0:P],
               [hb, ident], [bk])
        cp("act", hT.ap[:, :, 0:P], bf(bk)[:, 0:8 * P].rearrange("p (a t) -> p a t", a=8), [bk], [hT])

    def rms_a(xt, P):
        act(junk.ap[0:P, :], xt.ap[0:P, :], AF.Square, [xt], [junk, ssq], accum_out=ssq.ap[0:P, :])
        rsqrt_stat(rstd.ap, ssq.ap, 1, 1.0 / D, P, [ssq], rstd)
        stt("dve", hb.ap[0:P, :], xt.ap[0:P, :], rstd.ap[0:P, :], gB.ap[0:P, :], ALU.mult, ALU.mult,
            [xt, rstd, gB], [hb])

    def rms_b(P):
        bk = nb()
        for kc in range(8):
            tr(bf(bk)[:, kc * P:(kc + 1) * P], hb.ap[0:P, kc * 128:(kc + 1) * 128], ident.ap[0:P, 0:P],
               [hb, ident], [bk])
        cp("act", hT.ap[:, :, 0:P], bf(bk)[:, 0:8 * P].rearrange("p (a t) -> p a t", a=8), [bk], [hT])

    def out_proj_a(og, ogT, nec, P):
        half = nec // 2
        for hh in range(2):
            bk = nb()
            for e_ in range(half):
                ec = hh * half + e_
                tr(bf(bk)[:, e_ * P:(e_ + 1) * P], og.ap[0:P, ec * 128:(ec + 1) * 128], ident.ap[0:P, 0:P],
                   [og, ident], [bk])
            cp("act" if hh == 0 else "dve", ogT.ap[:, hh * half:(hh + 1) * half, 0:P],
               bf(bk)[:, 0:half * P].rearrange("p (a t) -> p a t", a=half), [bk], [ogT])

    def out_proj_b(ogT, nec, P, xt):
        for n in range(2):
            bk = nb()
            for ec in range(nec):
                mm(bk.ap[0:P, :], ogT.ap[:, ec, 0:P], Wout.ap[:, ec, n * 512:(n + 1) * 512], ec == 0, ec == nec - 1,
                   [ogT, Wout], [bk])
            tt("dve", xt.ap[0:P, n * 512:(n + 1) * 512], bk.ap[0:P, :], xt.ap[0:P, n * 512:(n + 1) * 512], ALU.add,
               [bk, xt], [xt])

    def proj(P, n0, width):
        bk = nb()
        wr = [Wbig, WG[n0 // 512]] if wgroups[0] else [Wbig]
        for kc in range(8):
            mm(bk.ap[0:P, 0:width], hT.ap[:, kc, 0:P], Wbig.ap[:, kc, n0:n0 + width], kc == 0, kc == 7,
               [hT] + wr, [bk])
        return bk

    def out_proj(og, ogT, nec, P, xt):
        half = nec // 2
        for hh in range(2):
            bk = nb()
            for e_ in range(half):
                ec = hh * half + e_
                tr(bf(bk)[:, e_ * P:(e_ + 1) * P], og.ap[0:P, ec * 128:(ec + 1) * 128], ident.ap[0:P, 0:P],
                   [og, ident], [bk])
            cp("act" if hh == 0 else "dve", ogT.ap[:, hh * half:(hh + 1) * half, 0:P],
               bf(bk)[:, 0:half * P].rearrange("p (a t) -> p a t", a=half), [bk], [ogT])
        for n in range(2):
            bk = nb()
            for ec in range(nec):
                mm(bk.ap[0:P, :], ogT.ap[:, ec, 0:P], Wout.ap[:, ec, n * 512:(n + 1) * 512], ec == 0, ec == nec - 1,
                   [ogT, Wout], [bk])
            tt("dve", xt.ap[0:P, n * 512:(n + 1) * 512], bk.ap[0:P, :], xt.ap[0:P, n * 512:(n + 1) * 512], ALU.add,
               [bk, xt], [xt])

    WG = [Res(f"wg{n}") for n in range(12)]
    wgroups = [False]

    def load_w_in(w_in, F_, by_group=False):
        if by_group:
            w3 = w_in.rearrange("(kc p) f -> p kc f", p=128)
            for n in range(F_ // 512):
                S.dma("pool", Wbig.ap[:, :, n * 512:(n + 1) * 512], w3[:, :, n * 512:(n + 1) * 512], [], [WG[n]],
                      grp=f"wg{n}")
            return
        for kc in range(8):
            S.dma("pool", Wbig.ap[:, kc, 0:F_], w_in[kc * 128:(kc + 1) * 128, :], [], [Wbig] + WG, grp="w")

    def prefetch_next_w_in(layer):
        nl = layer + 1
        if nl >= DEPTH:
            return
        if nl % 2 == 0:
            load_w_in(w_in_ret[nl // 2], RIN)
        else:
            load_w_in(w_in_swa[nl // 2], SIN)

    def load_weights(w_in, F_, w_out, nec, gain, skip_in=False, by_group=False):
        wgroups[0] = by_group
        if not skip_in:
            load_w_in(w_in, F_, by_group)
        for ec in range(nec):
            S.dma("pool", Wout.ap[:, ec, :], w_out[ec * 128:(ec + 1) * 128, :], [], [Wout], grp="w")
        S.dma("sp", gB.ap, gain.partition_broadcast(128), [], [gB])

    S.dma("sp", CST.ap, cst[:, 0:CWS], [], [CST])
    cp("dve", ident.ap, cs("ident"), [CST], [ident])
    cp("dve", onehotB.ap, cs("onehot").rearrange("p (a b) -> p a b", a=NS), [CST], [onehotB])

    srcs = [xp, xa, xbd, xa]
    dsts = [xa, xbd, xa, yp]
    if DEPTH < 4:
        dsts[DEPTH - 1] = yp

    def stop_at(level):
        if STOP[0] == level:
            raise _Stop()

    try:
      stop_at(1)
      for layer in range(DEPTH):
        li = layer // 2
        src, dst = srcs[layer], dsts[layer]
        S.barrier()
        A.off = PH0
        if layer % 2 == 0:
            load_weights(w_in_ret[li], RIN, w_out_ret[li], 16, norm_ret[li], skip_in=layer > 0, by_group=(layer == 0))
            xb = [A([D], F32, name=f"xb{i}") for i in range(3)]
            gout = A([16], F32, name="gout")
            cqt = A([8 * 128], BF16, name="cqt")
            ckt = A([8 * 128], BF16, name="ckt")
            S.dma("pool", cqt.ap, cst[:, coff["cq"][0]:coff["cq"][0] + 1024], [], [cqt])
            S.dma("pool", ckt.ap, cst[:, coff["ck"][0]:coff["ck"][0] + 1024], [], [ckt])
            S.dma("sp", gout.ap, ret_out_norm[li].rearrange("(ec p) -> p ec", p=128), [], [gout],
                  allow_slow_non_contiguous=True)
            for ec in range(16):
                ts("dve", Wout.ap[:, ec, :], Wout.ap[:, ec, :], gout.ap[:, ec:ec + 1], None,
                   ALU.mult, None, [Wout, gout], [Wout])
            stop_at(2)
            qkog = A([RV], BF16, name="qkog")
            qtm = T(qkog.ap[:, 0:RQK], "x"); qtm.res = qkog.res
            ktm = T(qkog.ap[:, RQK:2 * RQK], "x"); ktm.res = qkog.res
            kdec = A([RQK], BF16, name="kdec")
            vtm = A([RV], BF16, name="vtm")
            gs = A([RV], BF16, name="gs")
            qTs = A([8, 128], BF16, name="qTs")
            kTs = A([8, 128], BF16, name="kTs")
            inT = A([4, 128], BF16, name="inT")
            og = A([RV], BF16, name="og")
            ogT = A([16, 128], BF16, name="ogT")
            Sf = [A([512], F32, name=f"Sf{j}") for j in range(8)]
            Sb = [A([512], BF16, name=f"Sb{j}") for j in range(8)]
            for j in range(8):
                memset("pool", Sf[j], 0.0)
                memset("pool", Sb[j], 0.0)
            PH1 = A.off

            def r_front_a(c):
                xt = xb[c % 3]
                S.dma("sp", xt.ap, src[c * 128:(c + 1) * 128, :], [], [xt], grp=f"xb{c % 3}")
                rms_a(xt, 128)

            def r_proj(c, n):
                bk = proj(128, n * 512, 512)
                if n < 2:
                    cp("act", qtm.ap[:, n * 512:(n + 1) * 512], bk.ap, [bk], [qtm])
                elif n < 4:
                    m = n - 2
                    cp("act", ktm.ap[:, m * 512:(m + 1) * 512], bk.ap, [bk], [ktm])
                    tt("dve", kdec.ap[:, m * 512:(m + 1) * 512].rearrange("p (a b) -> p a b", a=2),
                       bk.ap.rearrange("p (a b) -> p a b", a=2),
                       cs("kd")[:, 2 * m:2 * m + 2].unsqueeze(2).to_broadcast([128, 2, 256]), ALU.mult,
                       [bk, CST], [kdec])
                elif n < 8:
                    m = n - 4
                    cp("dve", vtm.ap[:, m * 512:(m + 1) * 512], bk.ap, [bk], [vtm])
                else:
                    m = n - 8
                    act(gs.ap[:, m * 512:(m + 1) * 512], bk.ap, AF.Silu, [bk], [gs])

            def r_qkT(c):
                for (srct, dstt, ctile) in ((qtm, qTs, cqt), (ktm, kTs, ckt)):
                    bk = nb()
                    for j in range(8):
                        tr(bf(bk)[:, j * 128:(j + 1) * 128], srct.ap[:, j * 128:(j + 1) * 128], ident.ap,
                           [srct, ident], [bk])
                    tt("dve", dstt.ap.rearrange("p a t -> p (a t)"), bf(bk)[:, 0:1024], ctile.ap, ALU.mult,
                       [bk, ctile], [dstt])

            def r_inner(c):
                bk = nb()
                for h in range(4):
                    for dc in range(2):
                        mm(bk.ap[:, h * 128:(h + 1) * 128], kTs.ap[:, h * 2 + dc, :], qTs.ap[:, h * 2 + dc, :],
                           dc == 0, dc == 1, [kTs, qTs], [bk])
                tt("dve", inT.ap, bk.ap.rearrange("p (a t) -> p a t", a=4),
                   cs("causal").unsqueeze(1).to_broadcast([128, 4, 128]), ALU.mult, [bk, CST], [inT])

            def r_dS(c):
                for h in range(4):
                    for dc in range(2):
                        j = h * 2 + dc
                        bk = nb()
                        mm(bk.ap, kdec.ap[:, h * 256 + dc * 128:h * 256 + (dc + 1) * 128],
                           vtm.ap[:, h * 512:(h + 1) * 512], True, True, [kdec, vtm], [bk])
                        stt("dve", Sf[j].ap, Sf[j].ap, GAM[h] ** 128, bk.ap, ALU.mult, ALU.add, [Sf[j], bk], [Sf[j]])

            obk = []

            def r_o(c):
                del obk[:]
                for h in range(4):
                    bk = nb()
                    pinned.add(banks.index(bk))
                    obk.append(bk)
                    mm(bk.ap, inT.ap[:, h, :], vtm.ap[:, h * 512:(h + 1) * 512], True, False, [inT, vtm], [bk])
                    for dc in range(2):
                        j = h * 2 + dc
                        mm(bk.ap, qTs.ap[:, j, :], Sb[j].ap, False, dc == 1, [qTs, Sb[j]], [bk])
                    act(junk.ap[:, 0:512], bk.ap, AF.Square, [bk], [junk, ssqo], accum_out=ssqo.ap[:, h:h + 1])
                if c < NCH - 1:
                    for j in range(8):
                        cp("pool", Sb[j].ap, Sf[j].ap, [Sf[j]], [Sb[j]])

            def r_og(c):
                rsqrt_stat(rstdo.ap, ssqo.ap, 4, 1.0 / RDV, 128, [ssqo], rstdo)
                for h in range(4):
                    stt("dve", og.ap[:, h * 512:(h + 1) * 512], obk[h].ap, rstdo.ap[:, h:h + 1],
                        gs.ap[:, h * 512:(h + 1) * 512], ALU.mult, ALU.mult, [obk[h], rstdo, gs], [og])
                    pinned.discard(banks.index(obk[h]))

            r_front_a(0)
            rms_b(128)
            for n in range(12):
                r_proj(0, n)
            r_qkT(0)
            for c in range(NCH):
                nxt = c + 1 < NCH
                xt = xb[c % 3]
                if nxt:
                    r_front_a(c + 1)
                r_inner(c)
                if nxt:
                    rms_b(128)
                r_dS(c)
                r_o(c)
                r_og(c)
                if nxt:
                    for n in range(12):
                        r_proj(c + 1, n)
                out_proj_a(og, ogT, 16, 128)
                if nxt:
                    r_qkT(c + 1)
                out_proj_b(ogT, 16, 128, xt)
                S.dma("pool", dst[c * 128:(c + 1) * 128, :], xt.ap, [xt], [], grp=f"xb{c % 3}")
            for j in range(8):
                h, dc = j // 2, j % 2
                S.dma("pool", rp[li, h, dc * 128:(dc + 1) * 128, :], Sf[j].ap, [Sf[j]], [])

            stop_at(4)
            S.barrier()
            A.off = PH0
            xst = A([D], F32, parts=NS, name="xst")
            S.dma("sp", xst.ap, xs if layer == 0 else xsd, [], [xst])
            q_s = A([RQK], BF16, parts=NS, name="q_s")
            k_s = A([RQK], BF16, parts=NS, name="k_s")
            v_s = A([RV], BF16, parts=NS, name="v_s")
            gs_s = A([RV], BF16, parts=NS, name="gs_s")
            qT_s = A([8, NS], BF16, name="qT_s")
            qTm = A([NS, 8, NS], BF16, name="qTm")
            km = [A([RQK], BF16, parts=NS, name=f"km{i}") for i in range(2)]
            S0 = [A([2, 512], F32, name=f"S0_{i}") for i in range(3)]
            Sn = [A([2, 512], F32, name=f"Sn_{i}") for i in range(3)]
            Sbs = [A([2, 512], BF16, name=f"Sbs_{i}") for i in range(2)]
            oacc = A([RV], F32, parts=NS, name="oacc")
            og_s = A([RV], BF16, parts=NS, name="og_s")
            ogT_s = A([16, NS], BF16, name="ogT_s")
            rmsnorm_to_hT(xst, NS)
            for n in range(12):
                bk = proj(NS, n * 512, 512)
                b16 = bk.ap[0:NS, :]
                if n < 2:
                    cp("act", q_s.ap[:, n * 512:(n + 1) * 512], b16, [bk], [q_s])
                elif n < 4:
                    m = n - 2
                    ts("dve", k_s.ap[:, m * 512:(m + 1) * 512], b16, RDK ** -0.5, None, ALU.mult, None, [bk], [k_s])
                elif n < 8:
                    m = n - 4
                    cp("dve", v_s.ap[:, m * 512:(m + 1) * 512], b16, [bk], [v_s])
                else:
                    m = n - 8
                    act(gs_s.ap[:, m * 512:(m + 1) * 512], b16, AF.Silu, [bk], [gs_s])
            prefetch_next_w_in(layer)
            bk = nb()
            for j in range(8):
                tr(bf(bk)[:, j * NS:(j + 1) * NS], q_s.ap[:, j * 128:(j + 1) * 128], ident.ap[0:NS, 0:NS],
                   [q_s, ident], [bk])
            cp("act", qT_s.ap, bf(bk)[:, 0:8 * NS].rearrange("p (a t) -> p a t", a=8), [bk], [qT_s])
            for b in range(NS):
                tt("dve", qTm.ap[:, b, :, :], qT_s.ap,
                   onehotB.ap[:, b, :].unsqueeze(1).to_broadcast([128, 8, NS]), ALU.mult, [qT_s, onehotB], [qTm])
            memset("pool", oacc, 0.0)
            it = 0
            pend = []

            def flush_one():
                bk_, h_ = pend.pop(0)
                tt("dve", oacc.ap[:, h_ * 512:(h_ + 1) * 512], bk_.ap[0:NS, :], oacc.ap[:, h_ * 512:(h_ + 1) * 512],
                   ALU.add, [bk_, oacc], [oacc])
                pinned.discard(banks.index(bk_))

            items = [(b_, h_) for b_ in range(NS) for h_ in range(RH)]

            def s_stage1(i):
                b_, h_ = items[i]
                kmb = km[b_ % 2]
                if h_ == 0:
                    ts("dve", kmb.ap, k_s.ap, cs("ident")[0:NS, b_:b_ + 1], None, ALU.mult, None, [k_s, CST], [kmb])
                s0 = S0[i % 3]
                S.dma("sp", s0.ap, st[li, b_, h_].rearrange("(dc p) e -> p dc e", p=128), [], [s0], grp=f"s0{i % 3}")
                kvb = []
                for dc in range(2):
                    bk = nb()
                    pinned.add(banks.index(bk))
                    mm(bk.ap, kmb.ap[:, h_ * 256 + dc * 128:h_ * 256 + (dc + 1) * 128],
                       v_s.ap[:, h_ * 512:(h_ + 1) * 512], True, True, [kmb, v_s], [bk])
                    kvb.append(bk)
                return kvb

            def s_stage2(i, kvb):
                b_, h_ = items[i]
                s0, sn, sb = S0[i % 3], Sn[i % 3], Sbs[i % 2]
                for dc in range(2):
                    stt("dve", sn.ap[:, dc, :], s0.ap[:, dc, :], GAM[h_], kvb[dc].ap, ALU.mult, ALU.add,
                        [s0, kvb[dc]], [sn])
                    pinned.discard(banks.index(kvb[dc]))
                S.dma("pool", rs[li, b_, h_].rearrange("(dc p) e -> p dc e", p=128), sn.ap, [sn], [], grp=f"sn{i % 3}")
                cp("act", sb.ap, sn.ap, [sn], [sb])
                bk = nb()
                for dc in range(2):
                    mm(bk.ap[0:NS, :], qTm.ap[:, b_, h_ * 2 + dc, :], sb.ap[:, dc, :], dc == 0, dc == 1,
                       [qTm, sb], [bk])
                pend.append((bk, h_))
                pinned.add(banks.index(bk))
                while len(pend) > 2:
                    flush_one()

            kv_next = s_stage1(0)
            for i in range(len(items)):
                kv_cur = kv_next
                if i + 1 < len(items):
                    kv_next = s_stage1(i + 1)
                s_stage2(i, kv_cur)
            while pend:
                flush_one()
            for h in range(RH):
                act(junk.ap[0:NS, 0:512], oacc.ap[:, h * 512:(h + 1) * 512], AF.Square, [oacc], [junk, ssqo],
                    accum_out=ssqo.ap[0:NS, h:h + 1])
            rsqrt_stat(rstdo.ap, ssqo.ap, 4, 1.0 / RDV, NS, [ssqo], rstdo)
            for h in range(RH):
                stt("dve", og_s.ap[:, h * 512:(h + 1) * 512], oacc.ap[:, h * 512:(h + 1) * 512],
                    rstdo.ap[0:NS, h:h + 1], gs_s.ap[:, h * 512:(h + 1) * 512], ALU.mult, ALU.mult,
                    [oacc, rstdo, gs_s], [og_s])
            out_proj(og_s, ogT_s, 16, NS, xst)
            S.dma("pool", ys if layer == DEPTH - 1 else xsd, xst.ap, [xst], [])
        else:
            load_weights(w_in_swa[li], SIN, w_out_swa[li], 8, norm_swa[li], skip_in=layer > 0)
            qg = A([SHD], F32, name="qg")
            kg = A([SHD], F32, name="kg")
            esink = A([SH], F32, name="esink")
            S.dma("sp", qg.ap, q_norm[li].partition_broadcast(128), [], [qg])
            S.dma("sp", kg.ap, k_norm[li].partition_broadcast(128), [], [kg])
            S.dma("sp", esink.ap, sinks[li].partition_broadcast(128), [], [esink])
            act(esink.ap, esink.ap, AF.Exp, [esink], [esink])
            sq = A([SQ + 128], F32, name="sq")
            PHS = A.off

            def spare(kc, off, n, name):
                assert SIN + off + 2 * n <= RIN
                t_ = T(Wbig.ap[:, kc, SIN + off:SIN + off + 2 * n].bitcast(F32), name)
                if name in A.rescache:
                    t_.res = A.rescache[name]
                else:
                    A.rescache[name] = t_.res
                return t_

            xb = [A([D], F32, name=f"xb{i}") for i in range(3)] + [spare(2, 0, D, "xb3")]
            Ec = A([SH * 2 * 128], F32, name="Ec")
            S.dma("sp", Ec.ap, cst[:, coff["E"][0]:coff["E"][0] + SH * 256], [], [Ec])
            qnP = A([SQ], BF16, name="qnP")
            kn_f = A([128], F32, name="kn_f")
            v_f = A([128], F32, name="v_f")
            kk = A([128], BF16, name="kk")
            kT = [A([128], BF16, name=f"kT{i}") for i in range(3)]
            vx = [A([2, 65], BF16, name=f"vx{i}") for i in range(3)]
            qT2 = [A([8, 128], BF16, name=f"qT{i}") for i in range(2)]
            Pf = [A([512], F32, name=f"Pf{i}") for i in range(2)]
            PT = [A([2, 2, 128], BF16, name=f"PT{i}") for i in range(8)]
            gs2 = [A([SQ], BF16, name=f"gs_w{i}") for i in range(2)]
            tmpo = A([SQ], F32, name="tmpo")
            og = A([SQ], BF16, name="og_w")
            ogT = A([8, 128], BF16, name="ogT_w")
            for i in range(3):
                memset("pool", vx[i], 1.0)
            Eap = Ec.ap

            def qk_norm(P, bq, bkv, qdst_perm, qdst_nat, kn_t, v_t):
                for n in range(2):
                    act(sq.ap[0:P, n * 512:(n + 1) * 512], bq[n].ap[0:P, :], AF.Square, [bq[n]], [sq])
                act(sq.ap[0:P, 1024:1152], bkv.ap[0:P, 0:128], AF.Square, [bkv], [sq])
                S.op("dve", lambda e: e.tensor_reduce(out=ssq18.ap[0:P, :], in_=sq.ap[0:P, :].rearrange("p (a b) -> p a b", b=64),
                                                      axis=AX.X, op=ALU.add), [sq], [ssq18])
                rsqrt_stat(rq18.ap, ssq18.ap, 18, 1.0 / SHD, P, [ssq18], rq18)
                for n in range(2):
                    src_ = bq[n].ap[0:P, :].rearrange("p (g d) -> p g d", g=8)
                    rb = rq18.ap[0:P, n * 8:(n + 1) * 8].unsqueeze(2).to_broadcast([P, 8, 64])
                    if qdst_perm is not None:
                        o_ = qdst_perm.ap[0:P, :].rearrange("p (g kh d) -> p kh g d", g=8, kh=2)[:, n, :, :]
                        tt("dve", o_, src_, rb, ALU.mult, [bq[n], rq18], [qdst_perm])
                    else:
                        o_ = qdst_nat.ap[0:P, n * 512:(n + 1) * 512].rearrange("p (g d) -> p g d", g=8)
                        tt("dve", o_, src_, rb, ALU.mult, [bq[n], rq18], [qdst_nat])
                        tt("dve", o_, o_, qg.ap[0:P, :].unsqueeze(1).to_broadcast([P, 8, 64]), ALU.mult,
                           [qdst_nat, qg], [qdst_nat])
                kv3 = kn_t.ap[0:P, :].rearrange("p (a d) -> p a d", a=2)
                tt("dve", kv3, bkv.ap[0:P, 0:128].rearrange("p (a d) -> p a d", a=2),
                   rq18.ap[0:P, 16:18].unsqueeze(2).to_broadcast([P, 2, 64]), ALU.mult, [bkv, rq18], [kn_t])
                tt("dve", kv3, kv3, kg.ap[0:P, :].unsqueeze(1).to_broadcast([P, 2, 64]), ALU.mult, [kn_t, kg], [kn_t])
                cp("act", v_t.ap[0:P, :], bkv.ap[0:P, 128:256], [bkv], [v_t])

            def w_front_a(c):
                xt = xb[c % 4]
                S.dma("sp", xt.ap, src[c * 128:(c + 1) * 128, :], [], [xt], grp=f"xb{c % 4}")
                rms_a(xt, 128)

            def w_projA(c):
                p3 = c % 3
                bq = [proj(128, 0, 512), proj(128, 512, 512)]
                bkv = proj(128, 1024, 256)
                qk_norm(128, bq, bkv, qnP, None, kn_f, v_f)
                tt("dve", kk.ap.rearrange("p (a d) -> p a d", a=2), kn_f.ap.rearrange("p (a d) -> p a d", a=2),
                   qg.ap.unsqueeze(1).to_broadcast([128, 2, 64]), ALU.mult, [kn_f, qg], [kk])
                cp("act", vx[p3].ap[:, :, 0:64], v_f.ap.rearrange("p (a d) -> p a d", a=2), [v_f], [vx[p3]])
                if c == NCH - 1:
                    S.dma("pool", kp[li], kn_f.ap, [kn_f], [])
                    S.dma("pool", vp[li], v_f.ap, [v_f], [])

            def w_projG(c):
                g_ = gs2[c % 2]
                for n in range(2):
                    bk = proj(128, 1280 + n * 512, 512)
                    act(g_.ap[:, n * 512:(n + 1) * 512], bk.ap, AF.Silu, [bk], [g_])

            def w_trA(c):
                q_ = qT2[c % 2]
                bk = nb()
                for j in range(8):
                    tr(bf(bk)[:, j * 128:(j + 1) * 128], qnP.ap[:, j * 128:(j + 1) * 128], ident.ap, [qnP, ident], [bk])
                cp("act", q_.ap.rearrange("p a t -> p (a t)"), bf(bk)[:, 0:1024], [bk], [q_])
                bk = nb()
                tr(bf(bk)[:, 0:128], kk.ap, ident.ap, [kk, ident], [bk])
                cp("dve", kT[c % 3].ap, bf(bk)[:, 0:128], [bk], [kT[c % 3]])

            def w_scores(c, pr):
                blks = [1] if c == 0 else [0, 1]
                q_ = qT2[c % 2]
                kh, j0 = (2 * pr) // 8, (2 * pr) % 8
                bk = nb()
                for blk in blks:
                    ktile = kT[c % 3] if blk == 1 else kT[(c - 1) % 3]
                    mm(bk.ap[:, blk * 256:(blk + 1) * 256], ktile.ap[kh * 64:(kh + 1) * 64, :],
                       q_.ap[kh * 64:(kh + 1) * 64, j0:j0 + 2, :], True, True, [ktile, q_], [bk])
                pf = Pf[pr % 2]
                lo = blks[0]
                e4 = Eap[:, pr * 512:(pr + 1) * 512].rearrange("p (a b t) -> p b a t", a=2, b=2)
                p4 = pf.ap.rearrange("p (b a t) -> p b a t", a=2, b=2)
                act(pf.ap[:, lo * 256:512], bk.ap[:, lo * 256:512], AF.Exp, [bk], [pf], scale=SHD ** -0.5)
                tt("dve", PT[pr].ap[:, lo:2, :, :], p4[:, lo:2, :, :], e4[:, lo:2, :, :], ALU.mult,
                   [pf, Ec], [PT[pr]])

            def w_pv(c):
                blks = [1] if c == 0 else [0, 1]
                g_ = gs2[c % 2]
                groups = [(0, 6), (6, 12), (12, 16)]
                for (h0, h1) in groups:
                    bk = nb()
                    for h in range(h0, h1):
                        kh = h // 8
                        for bi, blk in enumerate(blks):
                            vt = vx[c % 3] if blk == 1 else vx[(c - 1) % 3]
                            mm(bk.ap[:, (h - h0) * 65:(h - h0 + 1) * 65], PT[h // 2].ap[:, blk, h % 2, :],
                               vt.ap[:, kh, :], bi == 0, bi == len(blks) - 1, [PT[h // 2], vt], [bk])
                    nh = h1 - h0
                    o3 = bk.ap[:, 0:nh * 65].rearrange("p (a d) -> p a d", a=nh)
                    tt("dve", den.ap[:, h0:h1], o3[:, :, 64], esink.ap[:, h0:h1], ALU.add, [bk, esink], [den])
                    S.op("dve", lambda e, h0=h0, h1=h1: e.reciprocal(out=den.ap[:, h0:h1], in_=den.ap[:, h0:h1]), [den], [den])
                    tt("dve", tmpo.ap[:, h0 * 64:h1 * 64].rearrange("p (a d) -> p a d", a=nh), o3[:, :, 0:64],
                       den.ap[:, h0:h1].unsqueeze(2).to_broadcast([128, nh, 64]), ALU.mult, [bk, den], [tmpo])
                    tt("pool", og.ap[:, h0 * 64:h1 * 64], tmpo.ap[:, h0 * 64:h1 * 64], g_.ap[:, h0 * 64:h1 * 64],
                       ALU.mult, [tmpo, g_], [og])

            w_front_a(0)
            rms_b(128)
            w_projA(0)
            w_projG(0)
            w_trA(0)
            for c in range(NCH):
                nxt = c + 1 < NCH
                prv = c >= 1
                if nxt:
                    w_front_a(c + 1)
                w_scores(c, 0)
                w_scores(c, 1)
                if nxt:
                    rms_b(128)
                if prv:
                    out_proj_a(og, ogT, 8, 128)
                w_scores(c, 2)
                w_scores(c, 3)
                if nxt:
                    w_projA(c + 1)
                if prv:
                    out_proj_b(ogT, 8, 128, xb[(c - 1) % 4])
                    S.dma("pool", dst[(c - 1) * 128:c * 128, :], xb[(c - 1) % 4].ap, [xb[(c - 1) % 4]], [],
                          grp=f"xb{(c - 1) % 4}")
                w_scores(c, 4)
                w_scores(c, 5)
                w_scores(c, 6)
                w_scores(c, 7)
                if nxt:
                    w_projG(c + 1)
                w_pv(c)
                if nxt:
                    w_trA(c + 1)
            out_proj_a(og, ogT, 8, 128)
            out_proj_b(ogT, 8, 128, xb[(NCH - 1) % 4])
            S.dma("pool", dst[(NCH - 1) * 128:NCH * 128, :], xb[(NCH - 1) % 4].ap, [xb[(NCH - 1) % 4]], [],
                  grp=f"xb{(NCH - 1) % 4}")

            S.barrier()
            A.off = PHS
            xst = A([D], F32, parts=NS, name="xst")
            S.dma("sp", xst.ap, xs if layer == 0 else xsd, [], [xst])
            onehotR = A([NS, 128], BF16, parts=NS, name="onehotR")
            cp("dve", onehotR.ap, cs("ident")[0:NS, 0:NS].unsqueeze(2).to_broadcast([NS, NS, 128]), [CST], [onehotR])
            qn_s = A([SQ], BF16, parts=NS, name="qn_s")
            kn_s = A([128], F32, parts=NS, name="kn_s")
            vv_s = A([128], F32, parts=NS, name="vv_s")
            gs_s = A([SQ], BF16, parts=NS, name="gs_s2")
            prod = A([SQ], F32, name="prod")
            snew = A([SH], F32, parts=NS, name="snew")
            pnew = A([SH], F32, parts=NS, name="pnew")
            kc_t = [A([128], F32, name=f"kc{i}") for i in range(2)]
            vc_t = [A([128], F32, name=f"vc{i}") for i in range(2)]
            STt = A([SH], F32, name="STt")
            PTb = [A([SH], BF16, name=f"PTb{i}") for i in range(2)]
            rhsb = [A([SQ], BF16, name=f"rhsb{i}") for i in range(2)]
            oacc = A([SQ], F32, parts=NS, name="oacc2")
            dacc = A([SH], F32, parts=NS, name="dacc")
            og_s = A([SQ], BF16, parts=NS, name="og_s2")
            ogT_s = A([8, NS], BF16, name="ogT_s2")
            qnP_s = A([SQ], BF16, parts=NS, name="qnP_s")
            qTp = A([8, NS], BF16, name="qTp")
            qbd = A([NS, SH], BF16, name="qbd")
            kcb = [A([128], BF16, name=f"kcb{i}") for i in range(2)]
            KTb = [A([128], BF16, name=f"KTb{i}") for i in range(2)]
            rmsnorm_to_hT(xst, NS)
            bq = [proj(NS, 0, 512), proj(NS, 512, 512)]
            bkv = proj(NS, 1024, 256)
            qk_norm(NS, bq, bkv, None, qn_s, kn_s, vv_s)
            for n in range(2):
                bk = proj(NS, 1280 + n * 512, 512)
                act(gs_s.ap[:, n * 512:(n + 1) * 512], bk.ap[0:NS, :], AF.Silu, [bk], [gs_s])
            prefetch_next_w_in(layer)
            cp("dve", qnP_s.ap.rearrange("p (g kh d) -> p kh g d", g=8, kh=2),
               qn_s.ap.rearrange("p (kh g d) -> p kh g d", kh=2, g=8), [qn_s], [qnP_s])
            bk = nb()
            for g_ in range(8):
                tr(bf(bk)[:, g_ * NS:(g_ + 1) * NS], qnP_s.ap[:, g_ * 128:(g_ + 1) * 128], ident.ap[0:NS, 0:NS],
                   [qnP_s, ident], [bk])
            cp("act", qTp.ap, bf(bk)[:, 0:8 * NS].rearrange("p (a t) -> p a t", a=8), [bk], [qTp])
            memset("pool", qbd, 0.0)
            cp("dve", qbd.ap[0:64, :, 0:8], qTp.ap[0:64, :, :].rearrange("p g b -> p b g"), [qTp], [qbd])
            cp("dve", qbd.ap[64:128, :, 8:16], qTp.ap[64:128, :, :].rearrange("p g b -> p b g"), [qTp], [qbd])
            S.dma("sp", ks[li][:, 0:127, :], ckd[li][:, 1:128, :], [], [])
            S.dma("sp", vs[li][:, 0:127, :], cvd[li][:, 1:128, :], [], [])
            S.dma("pool", ks[li][:, 127, :], kn_s.ap, [kn_s], [])
            S.dma("pool", vs[li][:, 127, :], vv_s.ap, [vv_s], [])
            tt("dve", prod.ap[0:NS, :].rearrange("p (a g d) -> p a g d", a=2, g=8),
               qn_s.ap.rearrange("p (a g d) -> p a g d", a=2, g=8),
               kn_s.ap.rearrange("p (a d) -> p a d", a=2).unsqueeze(2).to_broadcast([NS, 2, 8, 64]), ALU.mult,
               [qn_s, kn_s], [prod])
            S.op("dve", lambda e: e.tensor_reduce(out=snew.ap, in_=prod.ap[0:NS, :].rearrange("p (h d) -> p h d", d=64),
                                                  axis=AX.X, op=ALU.add), [prod], [snew])
            act(pnew.ap, snew.ap, AF.Exp, [snew], [pnew], scale=SHD ** -0.5)
            memset("pool", oacc, 0.0)
            memset("pool", dacc, 0.0)
            def q_stage1(b):
                kct, vct, ptb = kc_t[b % 2], vc_t[b % 2], PTb[b % 2]
                kb_, kt_ = kcb[b % 2], KTb[b % 2]
                S.dma("sp", kct.ap, ckd[li, b], [], [kct], grp=f"kc{b % 2}")
                S.dma("sp", vct.ap, cvd[li, b], [], [vct], grp=f"vc{b % 2}")
                cp("act", kb_.ap, kct.ap, [kct], [kb_])
                bk = nb()
                tr(bf(bk)[:, 0:128], kb_.ap, ident.ap, [kb_, ident], [bk])
                cp("dve", kt_.ap, bf(bk)[:, 0:128], [bk], [kt_])
                bk = nb()
                mm(bk.ap[:, 0:SH], kt_.ap, qbd.ap[:, b, :], True, True, [kt_, qbd], [bk])
                stt("dve", STt.ap, bk.ap[:, 0:SH], SHD ** -0.5, cs("alibi"), ALU.mult, ALU.add, [bk, CST], [STt])
                act(ptb.ap, STt.ap, AF.Exp, [STt], [ptb])

            def q_stage2(b):
                vct, ptb, rb_ = vc_t[b % 2], PTb[b % 2], rhsb[b % 2]
                tt("dve", rb_.ap.rearrange("p (a g d) -> p a g d", a=2, g=8),
                   vct.ap.rearrange("p (a d) -> p a d", a=2).unsqueeze(2).to_broadcast([128, 2, 8, 64]),
                   ptb.ap.rearrange("p (a g) -> p a g", a=2).unsqueeze(3).to_broadcast([128, 2, 8, 64]), ALU.mult,
                   [vct, ptb], [rb_])
                for n in range(2):
                    bk = nb()
                    mm(bk.ap[0:NS, :], onehotB.ap[:, b, :], rb_.ap[:, n * 512:(n + 1) * 512], True, True,
                       [onehotB, rb_], [bk])
                    tt("dve", oacc.ap[:, n * 512:(n + 1) * 512], bk.ap[0:NS, :], oacc.ap[:, n * 512:(n + 1) * 512],
                       ALU.add, [bk, oacc], [oacc])
                bk = nb()
                mm(bk.ap[0:NS, 0:SH], onehotB.ap[:, b, :], ptb.ap, True, True, [onehotB, ptb], [bk])
                tt("dve", dacc.ap, bk.ap[0:NS, 0:SH], dacc.ap, ALU.add, [bk, dacc], [dacc])

            q_stage1(0)
            for b in range(NS):
                if b + 1 < NS:
                    q_stage1(b + 1)
                q_stage2(b)
            tt("dve", prod.ap[0:NS, :].rearrange("p (a g d) -> p a g d", a=2, g=8),
               vv_s.ap.rearrange("p (a d) -> p a d", a=2).unsqueeze(2).to_broadcast([NS, 2, 8, 64]),
               pnew.ap.rearrange("p (a g) -> p a g", a=2).unsqueeze(3).to_broadcast([NS, 2, 8, 64]), ALU.mult,
               [vv_s, pnew], [prod])
            tt("dve", oacc.ap, oacc.ap, prod.ap[0:NS, :], ALU.add, [oacc, prod], [oacc])
            tt("dve", dacc.ap, dacc.ap, pnew.ap, ALU.add, [dacc, pnew], [dacc])
            tt("dve", dacc.ap, dacc.ap, esink.ap[0:NS, :], ALU.add, [dacc, esink], [dacc])
            S.op("dve", lambda e: e.reciprocal(out=dacc.ap, in_=dacc.ap), [dacc], [dacc])
            tt("dve", oacc.ap.rearrange("p (h d) -> p h d", d=64), oacc.ap.rearrange("p (h d) -> p h d", d=64),
               dacc.ap.unsqueeze(2).to_broadcast([NS, SH, 64]), ALU.mult, [oacc, dacc], [oacc])
            tt("dve", og_s.ap, oacc.ap, gs_s.ap, ALU.mult, [oacc, gs_s], [og_s])
            out_proj(og_s, ogT_s, 8, NS, xst)
            S.dma("pool", ys if layer == DEPTH - 1 else xsd, xst.ap, [xst], [])
    except _Stop:
        pass

    S.barrier()

    sems = {}
    for e in ("pe", "act", "dve", "pool"):
        sems[e] = es.enter_context(nc.semaphore(f"s_{e}"))
    for i, r in enumerate(S.dres):
        r.dsem = es.enter_context(nc.semaphore(f"d{i}_{r.name}"))

    def semof(key):
        return key.dsem if isinstance(key, DSem) else sems[key]

    def replay(name, e):
        for item in S.prog[name]:
            if item[0] == "wait":
                e.wait_ge(semof(item[1]), item[2])
            elif item[0] == "op":
                item[1](e).then_inc(sems[name], 1)
            else:
                _, o_, i_, r_, kw = item
                e.dma_start(out=o_, in_=i_, **kw).then_inc(r_.dsem, 16)

    with nc.Block() as block:
        @block.tensor
        def _(e):
            replay("pe", e)

        @block.scalar
        def _(e):
            replay("act", e)

        @block.vector
        def _(e):
            replay("dve", e)

        @block.gpsimd
        def _(e):
            replay("pool", e)

        @block.sync
        def _(e):
            replay("sp", e)
    es.close()
    return nc


_NC_CACHE = {}


def _get_nc(NCH, DEPTH):
    key = (NCH, DEPTH)
    if key not in _NC_CACHE:
        _NC_CACHE[key] = build_nc(NCH, DEPTH)
    return _NC_CACHE[key]


def kernel(x_prompt, x_sample, state_ret, cache_swa_k, cache_swa_v,
           norm_ret, w_in_ret, ret_out_norm, w_out_ret,
           norm_swa, w_in_swa, q_norm, k_norm, sinks, w_out_swa, _depth=4):
    f = lambda a: np.ascontiguousarray(np.asarray(a, dtype=np.float32))
    x_prompt, x_sample, state_ret = f(x_prompt), f(x_sample), f(state_ret)
    cache_swa_k, cache_swa_v = f(cache_swa_k), f(cache_swa_v)
    B, L, _ = x_prompt.shape
    NCH = L // 128
    DEPTH = _depth
    nc = _get_nc(NCH, DEPTH)
    c, off, CW = _cst_layout()
    cst = np.concatenate([c[k] for k in CORDER], axis=1).astype(np.float32)
    n_cores = 8
    nb_ = x_sample.shape[0] // n_cores
    assert nb_ == NS
    shared = {
        "norm_ret": f(norm_ret), "w_in_ret": f(w_in_ret), "ret_out_norm": f(ret_out_norm),
        "w_out_ret": f(w_out_ret), "norm_swa": f(norm_swa), "w_in_swa": f(w_in_swa),
        "q_norm": f(q_norm), "k_norm": f(k_norm), "sinks": f(sinks).reshape(sinks.shape[0], SH),
        "w_out_swa": f(w_out_swa), "cst": cst,
    }
    NSWA = cache_swa_k.shape[0]
    in_maps = []
    for core in range(n_cores):
        bsl = slice(core * NS, (core + 1) * NS)
        m = dict(shared)
        m["xp"] = f(x_prompt[(core // 2) % B])
        m["xs"] = f(x_sample[bsl, 0, :])
        m["st"] = f(state_ret[:, bsl])
        m["ck"] = f(cache_swa_k[:, bsl].reshape(NSWA, NS, 128, 128))
        m["cv"] = f(cache_swa_v[:, bsl].reshape(NSWA, NS, 128, 128))
        in_maps.append(m)
    res = run_bass_kernel_spmd(nc, in_maps, core_ids=list(range(n_cores)))
    R = res.results
    NRET = (DEPTH + 1) // 2
    nswa = DEPTH // 2
    y_prompt = np.stack([R[2 * b]["yp"] for b in range(B)], 0)
    y_sample = np.concatenate([R[cidx]["ys"] for cidx in range(n_cores)], 0)[:, None, :]
    new_ret_prompt = np.stack([R[2 * b]["rp"][:NRET] for b in range(B)], 1)
    new_ret_sample = np.concatenate([R[cidx]["rs"][:NRET] for cidx in range(n_cores)], 1)
    kp = np.stack([R[2 * b]["kp"][:nswa] for b in range(B)], 1).reshape(nswa, B, 128, SKV, SHD)
    vp = np.stack([R[2 * b]["vp"][:nswa] for b in range(B)], 1).reshape(nswa, B, 128, SKV, SHD)
    ksn = np.concatenate([R[cidx]["ks"][:nswa] for cidx in range(n_cores)], 1).reshape(nswa, -1, 128, SKV, SHD)
    vsn = np.concatenate([R[cidx]["vs"][:nswa] for cidx in range(n_cores)], 1).reshape(nswa, -1, 128, SKV, SHD)
    return (y_prompt.astype(np.float32), y_sample.astype(np.float32), new_ret_prompt.astype(np.float32),
            new_ret_sample.astype(np.float32), kp.astype(np.float32), vp.astype(np.float32),
            ksn.astype(np.float32), vsn.astype(np.float32))
```

```python
import math
from contextlib import ExitStack

import numpy as np
import concourse.bass as bass
import concourse.mybir as mybir
from concourse.bass_utils import run_bass_kernel_spmd

F32 = mybir.dt.float32
BF16 = mybir.dt.bfloat16
ALU = mybir.AluOpType
AF = mybir.ActivationFunctionType
AX = mybir.AxisListType

D = 1024
RH, RDK, RDV = 4, 256, 512
RQK, RV = 1024, 2048
RIN = 6144
SH, SKV, SG, SHD = 16, 2, 8, 64
SQ, SKVW = 1024, 128
SIN = 2304
EPS = 1e-6
NS = 16
GAM = [1.0 - 2.0 ** (-5.0 - h) for h in range(RH)]
SLOPE = [2.0 ** (-8.0 * (h + 1) / SH) for h in range(SH)]


class DSem:
    __slots__ = ("name", "dsem", "dcnt")

    def __init__(self, name):
        self.name = name
        self.dsem = None
        self.dcnt = 0


class Res:
    __slots__ = ("name", "w", "r", "excl")

    def __init__(self, name):
        self.name = name
        self.w = None
        self.r = []
        self.excl = False


class T:
    __slots__ = ("ap", "res")

    def __init__(self, ap, name):
        self.ap = ap
        self.res = Res(name)


class Sched:
    ENG = ("pe", "act", "dve", "pool", "sp")

    def __init__(self):
        self.prog = {e: [] for e in self.ENG}
        self.cnt = {e: 0 for e in self.ENG}
        self.waited = {e: {} for e in self.ENG}
        self.dres = []
        self.dsems = {}

    def _deps(self, reads, writes):
        t = []
        for r in reads:
            if r.w is not None:
                t.append(r.w)
            if r.excl:
                t.extend(r.r)
        for w in writes:
            if w.w is not None:
                t.append(w.w)
            t.extend(w.r)
        return t

    def _upd(self, reads, writes, tok):
        for r in reads:
            r.r = [x for x in r.r if x[0] != tok[0]] + [tok]
        for w in writes:
            w.w = tok
            w.r = []

    def _need(self, eng, toks):
        for key, val in toks:
            if key == "pe" and eng == "pe":
                continue
            if isinstance(key, DSem):
                val = key.dcnt
            if self.waited[eng].get(key, 0) >= val:
                continue
            self.waited[eng][key] = val
            self.prog[eng].append(("wait", key, val))

    def op(self, eng, fn, reads=(), writes=()):
        reads = [x.res if isinstance(x, T) else x for x in reads]
        writes = [x.res if isinstance(x, T) else x for x in writes]
        self._need(eng, self._deps(reads, writes))
        self.cnt[eng] += 1
        self.prog[eng].append(("op", fn))
        self._upd(reads, writes, (eng, self.cnt[eng]))

    def dma(self, eng, out, in_, reads=(), writes=(), res=None, grp=None, **kw):
        reads = [x.res if isinstance(x, T) else x for x in reads]
        writes = [x.res if isinstance(x, T) else x for x in writes]
        kind = "sw" if eng == "pool" else "hw"
        if grp is None:
            grp = (writes + reads)[0].name if (writes or reads) else "misc" + str(len(self.dsems))
        key = (grp, kind)
        if key not in self.dsems:
            self.dsems[key] = DSem(f"{grp}_{kind}")
            self.dres.append(self.dsems[key])
        res = self.dsems[key]
        self._need(eng, self._deps(reads, writes))
        res.dcnt += 16
        self.prog[eng].append(("dma", out, in_, res, kw))
        self._upd(reads, writes, (res, res.dcnt))

    def barrier(self):
        for e in self.ENG:
            toks = [(o, self.cnt[o]) for o in self.ENG if o != e and o != "sp" and self.cnt[o] > 0]
            toks += [(r, r.dcnt) for r in self.dres]
            if e != "sp" and e != "pe" and self.cnt[e] > 0:
                toks.append((e, self.cnt[e]))
            self._need(e, toks)


def _consts():
    c = {}
    c["ident"] = np.eye(128, dtype=np.float32)
    t = np.arange(128, dtype=np.float64)
    cq = np.zeros((128, 8, 128), np.float64)
    ck = np.zeros((128, 8, 128), np.float64)
    kd = np.zeros((128, 4), np.float64)
    for h in range(RH):
        lg = math.log(GAM[h])
        for dc in range(2):
            cq[:, h * 2 + dc, :] = np.exp(lg * (t + 1.0))[None, :]
            ck[:, h * 2 + dc, :] = (np.exp(-lg * (t + 1.0)) * RDK ** -0.5)[None, :]
        kd[:, h] = np.exp(lg * (127.0 - t)) * RDK ** -0.5
    c["cq"] = cq.reshape(128, 1024).astype(np.float32)
    c["ck"] = ck.reshape(128, 1024).astype(np.float32)
    c["kd"] = kd.astype(np.float32)
    s = np.arange(128)[:, None]
    q = np.arange(128)[None, :]
    c["causal"] = (q >= s).astype(np.float32)
    E = np.zeros((128, SH, 2, 128), np.float64)
    for h in range(SH):
        dprev = 128 + q - s
        E[:, h, 0, :] = np.where(s >= q, np.exp(-SLOPE[h] * dprev), 0.0)
        dcur = q - s
        E[:, h, 1, :] = np.where(s <= q, np.exp(-SLOPE[h] * np.maximum(dcur, 0)), 0.0)
    c["E"] = E.reshape(128, SH * 2 * 128).astype(np.float32)
    al = np.zeros((128, SH), np.float64)
    for h in range(SH):
        al[:, h] = -SLOPE[h] * (128.0 - np.arange(128))
    c["alibi"] = al.astype(np.float32)
    oh = np.zeros((128, NS, NS), np.float32)
    for b in range(NS):
        oh[:, b, b] = 1.0
    c["onehot"] = oh.reshape(128, NS * NS)
    c["eps"] = np.full((128, 1), EPS, np.float32)
    return c


CORDER = ["ident", "kd", "causal", "alibi", "onehot", "eps", "cq", "ck", "E"]
CSMALL = ["ident", "kd", "causal", "alibi", "onehot", "eps"]


def _cst_layout():
    c = _consts()
    off = {}
    o = 0
    for k in CORDER:
        off[k] = (o, c[k].shape[1])
        o += c[k].shape[1]
    return c, off, o


class _Stop(Exception):
    pass


STOP = [0]


def build_nc(NCH, DEPTH=4):
    TT = NCH * 128
    NRET = (DEPTH + 1) // 2
    NSWA = DEPTH // 2
    nc = bass.Bass("TRN2", target_bir_lowering=False)
    S = Sched()
    _, coff, CW = _cst_layout()
    CWS = sum(coff[k][1] for k in CSMALL)

    def din(name, shape):
        return nc.dram_tensor(name, list(shape), F32, kind="ExternalInput").ap()

    def dout(name, shape):
        return nc.dram_tensor(name, list(shape), F32, kind="ExternalOutput").ap()

    xp = din("xp", [TT, D])
    xs = din("xs", [NS, D])
    st = din("st", [max(NRET, 1), NS, RH, RDK, RDV])
    ckd = din("ck", [max(NSWA, 1), NS, 128, 128])
    cvd = din("cv", [max(NSWA, 1), NS, 128, 128])
    norm_ret = din("norm_ret", [2, D])
    w_in_ret = din("w_in_ret", [2, D, RIN])
    ret_out_norm = din("ret_out_norm", [2, RV])
    w_out_ret = din("w_out_ret", [2, RV, D])
    norm_swa = din("norm_swa", [2, D])
    w_in_swa = din("w_in_swa", [2, D, SIN])
    q_norm = din("q_norm", [2, SHD])
    k_norm = din("k_norm", [2, SHD])
    sinks = din("sinks", [2, SH])
    w_out_swa = din("w_out_swa", [2, SQ, D])
    cst = din("cst", [128, CW])

    yp = dout("yp", [TT, D])
    ys = dout("ys", [NS, D])
    rp = dout("rp", [max(NRET, 1), RH, RDK, RDV])
    rs = dout("rs", [max(NRET, 1), NS, RH, RDK, RDV])
    kp = dout("kp", [max(NSWA, 1), 128, 128])
    vp = dout("vp", [max(NSWA, 1), 128, 128])
    ks = dout("ks", [max(NSWA, 1), NS, 128, 128])
    vs = dout("vs", [max(NSWA, 1), NS, 128, 128])
    xa = nc.dram_tensor("xa", [TT, D], F32, kind="Internal").ap()
    xbd = nc.dram_tensor("xbd", [TT, D], F32, kind="Internal").ap()

    TOTAL = 106400
    es = ExitStack()
    big = es.enter_context(nc.sbuf_tensor("big", [128, TOTAL], BF16))
    bigap = big[:]
    banks = []
    for i in range(8):
        pt = es.enter_context(nc.psum_tensor(f"ps{i}", [128, 512], F32))
        banks.append(T(pt[:], f"ps{i}"))
        banks[-1].res.excl = True

    class Alloc:
        def __init__(self):
            self.off = 0
            self.n = 0
            self.rescache = {}

        def __call__(self, shape, dtype=BF16, parts=128, name=None):
            n = 1
            for s_ in shape:
                n *= s_
            ne = n * (2 if dtype == F32 else 1)
            self.off = (self.off + 15) // 16 * 16
            assert self.off + ne <= TOTAL, ("SBUF overflow", name, self.off, ne)
            ap = bigap[0:parts, self.off:self.off + ne]
            self.off += ne
            if dtype == F32:
                ap = ap.bitcast(F32)
            if len(shape) == 2:
                ap = ap.rearrange("p (a b) -> p a b", a=shape[0])
            elif len(shape) == 3:
                ap = ap.rearrange("p (a b c) -> p a b c", a=shape[0], b=shape[1])
            self.n += 1
            t_ = T(ap, name or f"t{self.n}")
            if name is not None:
                if name in self.rescache:
                    t_.res = self.rescache[name]
                else:
                    self.rescache[name] = t_.res
            return t_

    A = Alloc()

    CST = A([CWS], F32, name="cst")

    def cs(k):
        o, w = coff[k]
        return CST.ap[:, o:o + w]

    ident = A([128], BF16, name="ident")
    onehotB = A([NS, NS], BF16, name="onehotB")
    Wbig = A([8, RIN], BF16, name="Wbig")
    Wout = A([16, D], BF16, name="Wout")
    gB = A([D], F32, name="gB")
    hb = A([D], BF16, name="hb")
    hT = A([8, 128], BF16, name="hT")
    junk = T(hb.ap, "junk")
    junk.res = hb.res
    stat = A([64], F32, name="stat")
    PH0 = A.off
    xsd = nc.dram_tensor("xsd", [NS, D], F32, kind="Internal").ap()

    bank_i = [0]
    pinned = set()

    def nb():
        while True:
            b = banks[bank_i[0] % 8]
            i = bank_i[0] % 8
            bank_i[0] += 1
            if i not in pinned:
                return b

    def bf(bank):
        return bank.ap.bitcast(BF16)

    def act(out, in_, func, reads, writes, **kw):
        S.op("act", lambda e: e.activation(out=out, in_=in_, func=func, **kw), reads, writes)

    def tt(eng, out, in0, in1, op, reads, writes):
        S.op(eng, lambda e: e.tensor_tensor(out=out, in0=in0, in1=in1, op=op), reads, writes)

    def ts(eng, out, in0, s1, s2, op0, op1, reads, writes):
        if op1 is None:
            S.op(eng, lambda e: e.tensor_scalar(out=out, in0=in0, scalar1=s1, scalar2=None, op0=op0), reads, writes)
        else:
            S.op(eng, lambda e: e.tensor_scalar(out=out, in0=in0, scalar1=s1, scalar2=s2, op0=op0, op1=op1), reads, writes)

    def stt(eng, out, in0, scalar, in1, op0, op1, reads, writes):
        S.op(eng, lambda e: e.scalar_tensor_tensor(out=out, in0=in0, scalar=scalar, in1=in1, op0=op0, op1=op1), reads, writes)

    def cp(eng, out, in_, reads, writes):
        if eng == "act":
            S.op("act", lambda e: e.copy(out=out, in_=in_), reads, writes)
        else:
            S.op(eng, lambda e: e.tensor_copy(out=out, in_=in_), reads, writes)

    def mm(out, lhsT, rhs, start, stop, reads, writes):
        S.op("pe", lambda e: e.matmul(out, lhsT=lhsT, rhs=rhs, start=start, stop=stop), reads, writes)

    def tr(out, in_, idn, reads, writes):
        S.op("pe", lambda e: e.transpose(out=out, in_=in_, identity=idn), reads, writes)

    def memset(eng, t, val):
        S.op(eng, lambda e: e.memset(t.ap, val), [], [t])

    def rsqrt_stat(dst, src, n, scale, P, reads, writes_t):
        act(dst[0:P, 0:n], src[0:P, 0:n], AF.Ln, reads + [CST], [writes_t],
            bias=cs("eps")[0:P, :], scale=scale)
        act(dst[0:P, 0:n], dst[0:P, 0:n], AF.Exp, [writes_t], [writes_t], scale=-0.5)

    def stat_t(lo, n, name):
        return T(stat.ap[:, lo:lo + n], name)

    ssq = stat_t(0, 1, "ssq")
    rstd = stat_t(1, 1, "rstd")
    ssqo = stat_t(2, 4, "ssqo")
    rstdo = stat_t(6, 4, "rstdo")
    ssq18 = stat_t(10, 18, "ssq18")
    rq18 = stat_t(28, 18, "rq18")
    den = stat_t(46, 16, "den")

    def rmsnorm_to_hT(xt, P, ):
        act(junk.ap[0:P, :], xt.ap[0:P, :], AF.Square, [xt], [junk, ssq], accum_out=ssq.ap[0:P, :])
        rsqrt_stat(rstd.ap, ssq.ap, 1, 1.0 / D, P, [ssq], rstd)
        stt("dve", hb.ap[0:P, :], xt.ap[0:P, :], rstd.ap[0:P, :], gB.ap[0:P, :], ALU.mult, ALU.mult,
            [xt, rstd, gB], [hb])
        bk = nb()
        for kc in range(8):
            tr(bf(bk)[:, kc * P:(kc + 1) * P], hb.ap[0:P, kc * 128:(kc + 1) * 128], ident.ap[0:P, 0:P],
               [hb, ident], [bk])
        cp("act", hT.ap[:, :, 0:P], bf(bk)[:, 0:8 * P].rearrange("p (a t) -> p a t", a=8), [bk], [hT])

    def rms_a(xt, P):
        act(junk.ap[0:P, :], xt.ap[0:P, :], AF.Square, [xt], [junk, ssq], accum_out=ssq.ap[0:P, :])
        rsqrt_stat(rstd.ap, ssq.ap, 1, 1.0 / D, P, [ssq], rstd)
        stt("dve", hb.ap[0:P, :], xt.ap[0:P, :], rstd.ap[0:P, :], gB.ap[0:P, :], ALU.mult, ALU.mult,
            [xt, rstd, gB], [hb])

    def rms_b(P):
        bk = nb()
        for kc in range(8):
            tr(bf(bk)[:, kc * P:(kc + 1) * P], hb.ap[0:P, kc * 128:(kc + 1) * 128], ident.ap[0:P, 0:P],
               [hb, ident], [bk])
        cp("act", hT.ap[:, :, 0:P], bf(bk)[:, 0:8 * P].rearrange("p (a t) -> p a t", a=8), [bk], [hT])

    def out_proj_a(og, ogT, nec, P):
        half = nec // 2
        for hh in range(2):
            bk = nb()
            for e_ in range(half):
                ec = hh * half + e_
                tr(bf(bk)[:, e_ * P:(e_ + 1) * P], og.ap[0:P, ec * 128:(ec + 1) * 128], ident.ap[0:P, 0:P],
                   [og, ident], [bk])
            cp("act" if hh == 0 else "dve", ogT.ap[:, hh * half:(hh + 1) * half, 0:P],
               bf(bk)[:, 0:half * P].rearrange("p (a t) -> p a t", a=half), [bk], [ogT])

    def out_proj_b(ogT, nec, P, xt):
        for n in range(2):
            bk = nb()
            for ec in range(nec):
                mm(bk.ap[0:P, :], ogT.ap[:, ec, 0:P], Wout.ap[:, ec, n * 512:(n + 1) * 512], ec == 0, ec == nec - 1,
                   [ogT, Wout], [bk])
            tt("dve", xt.ap[0:P, n * 512:(n + 1) * 512], bk.ap[0:P, :], xt.ap[0:P, n * 512:(n + 1) * 512], ALU.add,
               [bk, xt], [xt])

    def proj(P, n0, width):
        bk = nb()
        wr = [Wbig, WG[n0 // 512]] if wgroups[0] else [Wbig]
        for kc in range(8):
            mm(bk.ap[0:P, 0:width], hT.ap[:, kc, 0:P], Wbig.ap[:, kc, n0:n0 + width], kc == 0, kc == 7,
               [hT] + wr, [bk])
        return bk

    def out_proj(og, ogT, nec, P, xt):
        half = nec // 2
        for hh in range(2):
            bk = nb()
            for e_ in range(half):
                ec = hh * half + e_
                tr(bf(bk)[:, e_ * P:(e_ + 1) * P], og.ap[0:P, ec * 128:(ec + 1) * 128], ident.ap[0:P, 0:P],
                   [og, ident], [bk])
            cp("act" if hh == 0 else "dve", ogT.ap[:, hh * half:(hh + 1) * half, 0:P],
               bf(bk)[:, 0:half * P].rearrange("p (a t) -> p a t", a=half), [bk], [ogT])
        for n in range(2):
            bk = nb()
            for ec in range(nec):
                mm(bk.ap[0:P, :], ogT.ap[:, ec, 0:P], Wout.ap[:, ec, n * 512:(n + 1) * 512], ec == 0, ec == nec - 1,
                   [ogT, Wout], [bk])
            tt("dve", xt.ap[0:P, n * 512:(n + 1) * 512], bk.ap[0:P, :], xt.ap[0:P, n * 512:(n + 1) * 512], ALU.add,
               [bk, xt], [xt])

    WG = [Res(f"wg{n}") for n in range(12)]
    wgroups = [False]

    def load_w_in(w_in, F_, by_group=False):
        if by_group:
            w3 = w_in.rearrange("(kc p) f -> p kc f", p=128)
            for n in range(F_ // 512):
                S.dma("pool", Wbig.ap[:, :, n * 512:(n + 1) * 512], w3[:, :, n * 512:(n + 1) * 512], [], [WG[n]],
                      grp=f"wg{n}")
            return
        for kc in range(8):
            S.dma("pool", Wbig.ap[:, kc, 0:F_], w_in[kc * 128:(kc + 1) * 128, :], [], [Wbig] + WG, grp="w")

    def prefetch_next_w_in(layer):
        nl = layer + 1
        if nl >= DEPTH:
            return
        if nl % 2 == 0:
            load_w_in(w_in_ret[nl // 2], RIN)
        else:
            load_w_in(w_in_swa[nl // 2], SIN)

    def load_weights(w_in, F_, w_out, nec, gain, skip_in=False, by_group=False):
        wgroups[0] = by_group
        if not skip_in:
            load_w_in(w_in, F_, by_group)
        for ec in range(nec):
            S.dma("pool", Wout.ap[:, ec, :], w_out[ec * 128:(ec + 1) * 128, :], [], [Wout], grp="w")
        S.dma("sp", gB.ap, gain.partition_broadcast(128), [], [gB])

    S.dma("sp", CST.ap, cst[:, 0:CWS], [], [CST])
    cp("dve", ident.ap, cs("ident"), [CST], [ident])
    cp("dve", onehotB.ap, cs("onehot").rearrange("p (a b) -> p a b", a=NS), [CST], [onehotB])

    srcs = [xp, xa, xbd, xa]
    dsts = [xa, xbd, xa, yp]
    if DEPTH < 4:
        dsts[DEPTH - 1] = yp

    def stop_at(level):
        if STOP[0] == level:
            raise _Stop()

    try:
      stop_at(1)
      for layer in range(DEPTH):
        li = layer // 2
        src, dst = srcs[layer], dsts[layer]
        S.barrier()
        A.off = PH0
        if layer % 2 == 0:
            load_weights(w_in_ret[li], RIN, w_out_ret[li], 16, norm_ret[li], skip_in=layer > 0, by_group=(layer == 0))
            xb = [A([D], F32, name=f"xb{i}") for i in range(3)]
            gout = A([16], F32, name="gout")
            cqt = A([8 * 128], BF16, name="cqt")
            ckt = A([8 * 128], BF16, name="ckt")
            S.dma("pool", cqt.ap, cst[:, coff["cq"][0]:coff["cq"][0] + 1024], [], [cqt])
            S.dma("pool", ckt.ap, cst[:, coff["ck"][0]:coff["ck"][0] + 1024], [], [ckt])
            S.dma("sp", gout.ap, ret_out_norm[li].rearrange("(ec p) -> p ec", p=128), [], [gout],
                  allow_slow_non_contiguous=True)
            for ec in range(16):
                ts("dve", Wout.ap[:, ec, :], Wout.ap[:, ec, :], gout.ap[:, ec:ec + 1], None,
                   ALU.mult, None, [Wout, gout], [Wout])
            stop_at(2)
            qkog = A([RV], BF16, name="qkog")
            qtm = T(qkog.ap[:, 0:RQK], "x"); qtm.res = qkog.res
            ktm = T(qkog.ap[:, RQK:2 * RQK], "x"); ktm.res = qkog.res
            kdec = A([RQK], BF16, name="kdec")
            vtm = A([RV], BF16, name="vtm")
            gs = A([RV], BF16, name="gs")
            qTs = A([8, 128], BF16, name="qTs")
            kTs = A([8, 128], BF16, name="kTs")
            inT = A([4, 128], BF16, name="inT")
            og = A([RV], BF16, name="og")
            ogT = A([16, 128], BF16, name="ogT")
            Sf = [A([512], F32, name=f"Sf{j}") for j in range(8)]
            Sb = [A([512], BF16, name=f"Sb{j}") for j in range(8)]
            for j in range(8):
                memset("pool", Sf[j], 0.0)
                memset("pool", Sb[j], 0.0)
            PH1 = A.off

            def r_front_a(c):
                xt = xb[c % 3]
                S.dma("sp", xt.ap, src[c * 128:(c + 1) * 128, :], [], [xt], grp=f"xb{c % 3}")
                rms_a(xt, 128)

            def r_proj(c, n):
                bk = proj(128, n * 512, 512)
                if n < 2:
                    cp("act", qtm.ap[:, n * 512:(n + 1) * 512], bk.ap, [bk], [qtm])
                elif n < 4:
                    m = n - 2
                    cp("act", ktm.ap[:, m * 512:(m + 1) * 512], bk.ap, [bk], [ktm])
                    tt("dve", kdec.ap[:, m * 512:(m + 1) * 512].rearrange("p (a b) -> p a b", a=2),
                       bk.ap.rearrange("p (a b) -> p a b", a=2),
                       cs("kd")[:, 2 * m:2 * m + 2].unsqueeze(2).to_broadcast([128, 2, 256]), ALU.mult,
                       [bk, CST], [kdec])
                elif n < 8:
                    m = n - 4
                    cp("dve", vtm.ap[:, m * 512:(m + 1) * 512], bk.ap, [bk], [vtm])
                else:
                    m = n - 8
                    act(gs.ap[:, m * 512:(m + 1) * 512], bk.ap, AF.Silu, [bk], [gs])

            def r_qkT(c):
                for (srct, dstt, ctile) in ((qtm, qTs, cqt), (ktm, kTs, ckt)):
                    bk = nb()
                    for j in range(8):
                        tr(bf(bk)[:, j * 128:(j + 1) * 128], srct.ap[:, j * 128:(j + 1) * 128], ident.ap,
                           [srct, ident], [bk])
                    tt("dve", dstt.ap.rearrange("p a t -> p (a t)"), bf(bk)[:, 0:1024], ctile.ap, ALU.mult,
                       [bk, ctile], [dstt])

            def r_inner(c):
                bk = nb()
                for h in range(4):
                    for dc in range(2):
                        mm(bk.ap[:, h * 128:(h + 1) * 128], kTs.ap[:, h * 2 + dc, :], qTs.ap[:, h * 2 + dc, :],
                           dc == 0, dc == 1, [kTs, qTs], [bk])
                tt("dve", inT.ap, bk.ap.rearrange("p (a t) -> p a t", a=4),
                   cs("causal").unsqueeze(1).to_broadcast([128, 4, 128]), ALU.mult, [bk, CST], [inT])

            def r_dS(c):
                for h in range(4):
                    for dc in range(2):
                        j = h * 2 + dc
                        bk = nb()
                        mm(bk.ap, kdec.ap[:, h * 256 + dc * 128:h * 256 + (dc + 1) * 128],
                           vtm.ap[:, h * 512:(h + 1) * 512], True, True, [kdec, vtm], [bk])
                        stt("dve", Sf[j].ap, Sf[j].ap, GAM[h] ** 128, bk.ap, ALU.mult, ALU.add, [Sf[j], bk], [Sf[j]])

            obk = []

            def r_o(c):
                del obk[:]
                for h in range(4):
                    bk = nb()
                    pinned.add(banks.index(bk))
                    obk.append(bk)
                    mm(bk.ap, inT.ap[:, h, :], vtm.ap[:, h * 512:(h + 1) * 512], True, False, [inT, vtm], [bk])
                    for dc in range(2):
                        j = h * 2 + dc
                        mm(bk.ap, qTs.ap[:, j, :], Sb[j].ap, False, dc == 1, [qTs, Sb[j]], [bk])
                    act(junk.ap[:, 0:512], bk.ap, AF.Square, [bk], [junk, ssqo], accum_out=ssqo.ap[:, h:h + 1])
                if c < NCH - 1:
                    for j in range(8):
                        cp("pool", Sb[j].ap, Sf[j].ap, [Sf[j]], [Sb[j]])

            def r_og(c):
                rsqrt_stat(rstdo.ap, ssqo.ap, 4, 1.0 / RDV, 128, [ssqo], rstdo)
                for h in range(4):
                    stt("dve", og.ap[:, h * 512:(h + 1) * 512], obk[h].ap, rstdo.ap[:, h:h + 1],
                        gs.ap[:, h * 512:(h + 1) * 512], ALU.mult, ALU.mult, [obk[h], rstdo, gs], [og])
                    pinned.discard(banks.index(obk[h]))

            r_front_a(0)
            rms_b(128)
            for n in range(12):
                r_proj(0, n)
            r_qkT(0)
            if NCH > 1:
                r_front_a(1)
            for c in range(NCH):
                nxt = c + 1 < NCH
                xt = xb[c % 3]
                r_inner(c)
                if nxt:
                    rms_b(128)
                r_dS(c)
                r_o(c)
                r_og(c)
                if nxt:
                    for n in range(12):
                        r_proj(c + 1, n)
                out_proj_a(og, ogT, 16, 128)
                if nxt:
                    r_qkT(c + 1)
                out_proj_b(ogT, 16, 128, xt)
                S.dma("pool", dst[c * 128:(c + 1) * 128, :], xt.ap, [xt], [], grp=f"xb{c % 3}")
                if c + 2 < NCH:
                    r_front_a(c + 2)
            for j in range(8):
                h, dc = j // 2, j % 2
                S.dma("pool", rp[li, h, dc * 128:(dc + 1) * 128, :], Sf[j].ap, [Sf[j]], [])

            stop_at(4)
            S.barrier()
            A.off = PH0
            xst = A([D], F32, parts=NS, name="xst")
            S.dma("sp", xst.ap, xs if layer == 0 else xsd, [], [xst])
            q_s = A([RQK], BF16, parts=NS, name="q_s")
            k_s = A([RQK], BF16, parts=NS, name="k_s")
            v_s = A([RV], BF16, parts=NS, name="v_s")
            gs_s = A([RV], BF16, parts=NS, name="gs_s")
            qT_s = A([8, NS], BF16, name="qT_s")
            qTm = A([NS, 8, NS], BF16, name="qTm")
            km = [A([RQK], BF16, parts=NS, name=f"km{i}") for i in range(2)]
            S0 = [A([2, 512], F32, name=f"S0_{i}") for i in range(3)]
            Sn = [A([2, 512], F32, name=f"Sn_{i}") for i in range(3)]
            Sbs = [A([2, 512], BF16, name=f"Sbs_{i}") for i in range(2)]
            oacc = A([RV], F32, parts=NS, name="oacc")
            og_s = A([RV], BF16, parts=NS, name="og_s")
            ogT_s = A([16, NS], BF16, name="ogT_s")
            rmsnorm_to_hT(xst, NS)
            for n in range(12):
                bk = proj(NS, n * 512, 512)
                b16 = bk.ap[0:NS, :]
                if n < 2:
                    cp("act", q_s.ap[:, n * 512:(n + 1) * 512], b16, [bk], [q_s])
                elif n < 4:
                    m = n - 2
                    ts("dve", k_s.ap[:, m * 512:(m + 1) * 512], b16, RDK ** -0.5, None, ALU.mult, None, [bk], [k_s])
                elif n < 8:
                    m = n - 4
                    cp("dve", v_s.ap[:, m * 512:(m + 1) * 512], b16, [bk], [v_s])
                else:
                    m = n - 8
                    act(gs_s.ap[:, m * 512:(m + 1) * 512], b16, AF.Silu, [bk], [gs_s])
            prefetch_next_w_in(layer)
            bk = nb()
            for j in range(8):
                tr(bf(bk)[:, j * NS:(j + 1) * NS], q_s.ap[:, j * 128:(j + 1) * 128], ident.ap[0:NS, 0:NS],
                   [q_s, ident], [bk])
            cp("act", qT_s.ap, bf(bk)[:, 0:8 * NS].rearrange("p (a t) -> p a t", a=8), [bk], [qT_s])
            for b in range(NS):
                tt("dve", qTm.ap[:, b, :, :], qT_s.ap,
                   onehotB.ap[:, b, :].unsqueeze(1).to_broadcast([128, 8, NS]), ALU.mult, [qT_s, onehotB], [qTm])
            memset("pool", oacc, 0.0)
            it = 0
            pend = []

            def flush_one():
                bk_, h_ = pend.pop(0)
                tt("dve", oacc.ap[:, h_ * 512:(h_ + 1) * 512], bk_.ap[0:NS, :], oacc.ap[:, h_ * 512:(h_ + 1) * 512],
                   ALU.add, [bk_, oacc], [oacc])
                pinned.discard(banks.index(bk_))

            items = [(b_, h_) for b_ in range(NS) for h_ in range(RH)]

            def s_stage1(i):
                b_, h_ = items[i]
                kmb = km[b_ % 2]
                if h_ == 0:
                    ts("dve", kmb.ap, k_s.ap, cs("ident")[0:NS, b_:b_ + 1], None, ALU.mult, None, [k_s, CST], [kmb])
                s0 = S0[i % 3]
                S.dma("sp", s0.ap, st[li, b_, h_].rearrange("(dc p) e -> p dc e", p=128), [], [s0], grp=f"s0{i % 3}")
                kvb = []
                for dc in range(2):
                    bk = nb()
                    pinned.add(banks.index(bk))
                    mm(bk.ap, kmb.ap[:, h_ * 256 + dc * 128:h_ * 256 + (dc + 1) * 128],
                       v_s.ap[:, h_ * 512:(h_ + 1) * 512], True, True, [kmb, v_s], [bk])
                    kvb.append(bk)
                return kvb

            def s_stage2(i, kvb):
                b_, h_ = items[i]
                s0, sn, sb = S0[i % 3], Sn[i % 3], Sbs[i % 2]
                for dc in range(2):
                    stt("dve", sn.ap[:, dc, :], s0.ap[:, dc, :], GAM[h_], kvb[dc].ap, ALU.mult, ALU.add,
                        [s0, kvb[dc]], [sn])
                    pinned.discard(banks.index(kvb[dc]))
                S.dma("pool", rs[li, b_, h_].rearrange("(dc p) e -> p dc e", p=128), sn.ap, [sn], [], grp=f"sn{i % 3}")
                cp("act", sb.ap, sn.ap, [sn], [sb])
                bk = nb()
                for dc in range(2):
                    mm(bk.ap[0:NS, :], qTm.ap[:, b_, h_ * 2 + dc, :], sb.ap[:, dc, :], dc == 0, dc == 1,
                       [qTm, sb], [bk])
                pend.append((bk, h_))
                pinned.add(banks.index(bk))
                while len(pend) > 2:
                    flush_one()

            kv_next = s_stage1(0)
            for i in range(len(items)):
                kv_cur = kv_next
                if i + 1 < len(items):
                    kv_next = s_stage1(i + 1)
                s_stage2(i, kv_cur)
            while pend:
                flush_one()
            for h in range(RH):
                act(junk.ap[0:NS, 0:512], oacc.ap[:, h * 512:(h + 1) * 512], AF.Square, [oacc], [junk, ssqo],
                    accum_out=ssqo.ap[0:NS, h:h + 1])
            rsqrt_stat(rstdo.ap, ssqo.ap, 4, 1.0 / RDV, NS, [ssqo], rstdo)
            for h in range(RH):
                stt("dve", og_s.ap[:, h * 512:(h + 1) * 512], oacc.ap[:, h * 512:(h + 1) * 512],
                    rstdo.ap[0:NS, h:h + 1], gs_s.ap[:, h * 512:(h + 1) * 512], ALU.mult, ALU.mult,
                    [oacc, rstdo, gs_s], [og_s])
            out_proj(og_s, ogT_s, 16, NS, xst)
            S.dma("pool", ys if layer == DEPTH - 1 else xsd, xst.ap, [xst], [])
        else:
            load_weights(w_in_swa[li], SIN, w_out_swa[li], 8, norm_swa[li], skip_in=layer > 0)
            qg = A([SHD], F32, name="qg")
            kg = A([SHD], F32, name="kg")
            esink = A([SH], F32, name="esink")
            S.dma("sp", qg.ap, q_norm[li].partition_broadcast(128), [], [qg])
            S.dma("sp", kg.ap, k_norm[li].partition_broadcast(128), [], [kg])
            S.dma("sp", esink.ap, sinks[li].partition_broadcast(128), [], [esink])
            act(esink.ap, esink.ap, AF.Exp, [esink], [esink])
            sq = A([SQ + 128], F32, name="sq")
            PHS = A.off

            def spare(kc, off, n, name):
                assert SIN + off + 2 * n <= RIN
                t_ = T(Wbig.ap[:, kc, SIN + off:SIN + off + 2 * n].bitcast(F32), name)
                if name in A.rescache:
                    t_.res = A.rescache[name]
                else:
                    A.rescache[name] = t_.res
                return t_

            xb = [A([D], F32, name=f"xb{i}") for i in range(3)] + [spare(2, 0, D, "xb3")]
            Ec = A([SH * 2 * 128], F32, name="Ec")
            S.dma("sp", Ec.ap, cst[:, coff["E"][0]:coff["E"][0] + SH * 256], [], [Ec])
            qnP = A([SQ], BF16, name="qnP")
            kn_f = A([128], F32, name="kn_f")
            v_f = A([128], F32, name="v_f")
            kk = A([128], BF16, name="kk")
            kT = [A([128], BF16, name=f"kT{i}") for i in range(3)]
            vx = [A([2, 65], BF16, name=f"vx{i}") for i in range(3)]
            qT2 = [A([8, 128], BF16, name=f"qT{i}") for i in range(2)]
            Pf = [A([512], F32, name=f"Pf{i}") for i in range(2)]
            PT = [A([2, 2, 128], BF16, name=f"PT{i}") for i in range(8)]
            gs2 = [A([SQ], BF16, name=f"gs_w{i}") for i in range(2)]
            tmpo = A([SQ], F32, name="tmpo")
            og = A([SQ], BF16, name="og_w")
            ogT = A([8, 128], BF16, name="ogT_w")
            for i in range(3):
                memset("pool", vx[i], 1.0)
            Eap = Ec.ap

            def qk_norm(P, bq, bkv, qdst_perm, qdst_nat, kn_t, v_t):
                for n in range(2):
                    act(sq.ap[0:P, n * 512:(n + 1) * 512], bq[n].ap[0:P, :], AF.Square, [bq[n]], [sq])
                act(sq.ap[0:P, 1024:1152], bkv.ap[0:P, 0:128], AF.Square, [bkv], [sq])
                S.op("dve", lambda e: e.tensor_reduce(out=ssq18.ap[0:P, :], in_=sq.ap[0:P, :].rearrange("p (a b) -> p a b", b=64),
                                                      axis=AX.X, op=ALU.add), [sq], [ssq18])
                rsqrt_stat(rq18.ap, ssq18.ap, 18, 1.0 / SHD, P, [ssq18], rq18)
                for n in range(2):
                    src_ = bq[n].ap[0:P, :].rearrange("p (g d) -> p g d", g=8)
                    rb = rq18.ap[0:P, n * 8:(n + 1) * 8].unsqueeze(2).to_broadcast([P, 8, 64])
                    if qdst_perm is not None:
                        o_ = qdst_perm.ap[0:P, :].rearrange("p (g kh d) -> p kh g d", g=8, kh=2)[:, n, :, :]
                        tt("dve", o_, src_, rb, ALU.mult, [bq[n], rq18], [qdst_perm])
                    else:
                        o_ = qdst_nat.ap[0:P, n * 512:(n + 1) * 512].rearrange("p (g d) -> p g d", g=8)
                        tt("dve", o_, src_, rb, ALU.mult, [bq[n], rq18], [qdst_nat])
                        tt("dve", o_, o_, qg.ap[0:P, :].unsqueeze(1).to_broadcast([P, 8, 64]), ALU.mult,
                           [qdst_nat, qg], [qdst_nat])
                kv3 = kn_t.ap[0:P, :].rearrange("p (a d) -> p a d", a=2)
                tt("dve", kv3, bkv.ap[0:P, 0:128].rearrange("p (a d) -> p a d", a=2),
                   rq18.ap[0:P, 16:18].unsqueeze(2).to_broadcast([P, 2, 64]), ALU.mult, [bkv, rq18], [kn_t])
                tt("dve", kv3, kv3, kg.ap[0:P, :].unsqueeze(1).to_broadcast([P, 2, 64]), ALU.mult, [kn_t, kg], [kn_t])
                cp("act", v_t.ap[0:P, :], bkv.ap[0:P, 128:256], [bkv], [v_t])

            def w_front_a(c):
                xt = xb[c % 4]
                S.dma("sp", xt.ap, src[c * 128:(c + 1) * 128, :], [], [xt], grp=f"xb{c % 4}")
                rms_a(xt, 128)

            def w_projA(c):
                p3 = c % 3
                bq = [proj(128, 0, 512), proj(128, 512, 512)]
                bkv = proj(128, 1024, 256)
                qk_norm(128, bq, bkv, qnP, None, kn_f, v_f)
                tt("dve", kk.ap.rearrange("p (a d) -> p a d", a=2), kn_f.ap.rearrange("p (a d) -> p a d", a=2),
                   qg.ap.unsqueeze(1).to_broadcast([128, 2, 64]), ALU.mult, [kn_f, qg], [kk])
                cp("act", vx[p3].ap[:, :, 0:64], v_f.ap.rearrange("p (a d) -> p a d", a=2), [v_f], [vx[p3]])
                if c == NCH - 1:
                    S.dma("pool", kp[li], kn_f.ap, [kn_f], [])
                    S.dma("pool", vp[li], v_f.ap, [v_f], [])

            def w_projG(c):
                g_ = gs2[c % 2]
                for n in range(2):
                    bk = proj(128, 1280 + n * 512, 512)
                    act(g_.ap[:, n * 512:(n + 1) * 512], bk.ap, AF.Silu, [bk], [g_])

            def w_trA(c):
                q_ = qT2[c % 2]
                bk = nb()
                for j in range(8):
                    tr(bf(bk)[:, j * 128:(j + 1) * 128], qnP.ap[:, j * 128:(j + 1) * 128], ident.ap, [qnP, ident], [bk])
                cp("act", q_.ap.rearrange("p a t -> p (a t)"), bf(bk)[:, 0:1024], [bk], [q_])
                bk = nb()
                tr(bf(bk)[:, 0:128], kk.ap, ident.ap, [kk, ident], [bk])
                cp("dve", kT[c % 3].ap, bf(bk)[:, 0:128], [bk], [kT[c % 3]])

            def w_scores(c, pr):
                blks = [1] if c == 0 else [0, 1]
                q_ = qT2[c % 2]
                kh, j0 = (2 * pr) // 8, (2 * pr) % 8
                bk = nb()
                for blk in blks:
                    ktile = kT[c % 3] if blk == 1 else kT[(c - 1) % 3]
                    mm(bk.ap[:, blk * 256:(blk + 1) * 256], ktile.ap[kh * 64:(kh + 1) * 64, :],
                       q_.ap[kh * 64:(kh + 1) * 64, j0:j0 + 2, :], True, True, [ktile, q_], [bk])
                pf = Pf[pr % 2]
                lo = blks[0]
                e4 = Eap[:, pr * 512:(pr + 1) * 512].rearrange("p (a b t) -> p b a t", a=2, b=2)
                p4 = pf.ap.rearrange("p (b a t) -> p b a t", a=2, b=2)
                act(pf.ap[:, lo * 256:512], bk.ap[:, lo * 256:512], AF.Exp, [bk], [pf], scale=SHD ** -0.5)
                tt("dve", PT[pr].ap[:, lo:2, :, :], p4[:, lo:2, :, :], e4[:, lo:2, :, :], ALU.mult,
                   [pf, Ec], [PT[pr]])

            def w_pv(c):
                blks = [1] if c == 0 else [0, 1]
                g_ = gs2[c % 2]
                groups = [(0, 6), (6, 12), (12, 16)]
                for (h0, h1) in groups:
                    bk = nb()
                    for h in range(h0, h1):
                        kh = h // 8
                        for bi, blk in enumerate(blks):
                            vt = vx[c % 3] if blk == 1 else vx[(c - 1) % 3]
                            mm(bk.ap[:, (h - h0) * 65:(h - h0 + 1) * 65], PT[h // 2].ap[:, blk, h % 2, :],
                               vt.ap[:, kh, :], bi == 0, bi == len(blks) - 1, [PT[h // 2], vt], [bk])
                    nh = h1 - h0
                    o3 = bk.ap[:, 0:nh * 65].rearrange("p (a d) -> p a d", a=nh)
                    tt("dve", den.ap[:, h0:h1], o3[:, :, 64], esink.ap[:, h0:h1], ALU.add, [bk, esink], [den])
                    S.op("dve", lambda e, h0=h0, h1=h1: e.reciprocal(out=den.ap[:, h0:h1], in_=den.ap[:, h0:h1]), [den], [den])
                    tt("dve", tmpo.ap[:, h0 * 64:h1 * 64].rearrange("p (a d) -> p a d", a=nh), o3[:, :, 0:64],
                       den.ap[:, h0:h1].unsqueeze(2).to_broadcast([128, nh, 64]), ALU.mult, [bk, den], [tmpo])
                    tt("pool", og.ap[:, h0 * 64:h1 * 64], tmpo.ap[:, h0 * 64:h1 * 64], g_.ap[:, h0 * 64:h1 * 64],
                       ALU.mult, [tmpo, g_], [og])

            w_front_a(0)
            rms_b(128)
            w_projA(0)
            w_projG(0)
            w_trA(0)
            for c in range(NCH):
                nxt = c + 1 < NCH
                prv = c >= 1
                if nxt:
                    w_front_a(c + 1)
                w_scores(c, 0)
                w_scores(c, 1)
                if nxt:
                    rms_b(128)
                if prv:
                    out_proj_a(og, ogT, 8, 128)
                w_scores(c, 2)
                w_scores(c, 3)
                if nxt:
                    w_projA(c + 1)
                if prv:
                    out_proj_b(ogT, 8, 128, xb[(c - 1) % 4])
                    S.dma("pool", dst[(c - 1) * 128:c * 128, :], xb[(c - 1) % 4].ap, [xb[(c - 1) % 4]], [],
                          grp=f"xb{(c - 1) % 4}")
                w_scores(c, 4)
                w_scores(c, 5)
                w_scores(c, 6)
                w_scores(c, 7)
                if nxt:
                    w_projG(c + 1)
                w_pv(c)
                if nxt:
                    w_trA(c + 1)
            out_proj_a(og, ogT, 8, 128)
            out_proj_b(ogT, 8, 128, xb[(NCH - 1) % 4])
            S.dma("pool", dst[(NCH - 1) * 128:NCH * 128, :], xb[(NCH - 1) % 4].ap, [xb[(NCH - 1) % 4]], [],
                  grp=f"xb{(NCH - 1) % 4}")

            S.barrier()
            A.off = PHS
            xst = A([D], F32, parts=NS, name="xst")
            S.dma("sp", xst.ap, xs if layer == 0 else xsd, [], [xst])
            onehotR = A([NS, 128], BF16, parts=NS, name="onehotR")
            cp("dve", onehotR.ap, cs("ident")[0:NS, 0:NS].unsqueeze(2).to_broadcast([NS, NS, 128]), [CST], [onehotR])
            qn_s = A([SQ], BF16, parts=NS, name="qn_s")
            kn_s = A([128], F32, parts=NS, name="kn_s")
            vv_s = A([128], F32, parts=NS, name="vv_s")
            gs_s = A([SQ], BF16, parts=NS, name="gs_s2")
            prod = A([SQ], F32, name="prod")
            snew = A([SH], F32, parts=NS, name="snew")
            pnew = A([SH], F32, parts=NS, name="pnew")
            kc_t = [A([128], F32, name=f"kc{i}") for i in range(2)]
            vc_t = [A([128], F32, name=f"vc{i}") for i in range(2)]
            STt = A([SH], F32, name="STt")
            PTb = [A([SH], BF16, name=f"PTb{i}") for i in range(2)]
            rhsb = [A([SQ], BF16, name=f"rhsb{i}") for i in range(2)]
            oacc = A([SQ], F32, parts=NS, name="oacc2")
            dacc = A([SH], F32, parts=NS, name="dacc")
            og_s = A([SQ], BF16, parts=NS, name="og_s2")
            ogT_s = A([8, NS], BF16, name="ogT_s2")
            qnP_s = A([SQ], BF16, parts=NS, name="qnP_s")
            qTp = A([8, NS], BF16, name="qTp")
            qbd = A([NS, SH], BF16, name="qbd")
            kcb = [A([128], BF16, name=f"kcb{i}") for i in range(2)]
            KTb = [A([128], BF16, name=f"KTb{i}") for i in range(2)]
            rmsnorm_to_hT(xst, NS)
            bq = [proj(NS, 0, 512), proj(NS, 512, 512)]
            bkv = proj(NS, 1024, 256)
            qk_norm(NS, bq, bkv, None, qn_s, kn_s, vv_s)
            for n in range(2):
                bk = proj(NS, 1280 + n * 512, 512)
                act(gs_s.ap[:, n * 512:(n + 1) * 512], bk.ap[0:NS, :], AF.Silu, [bk], [gs_s])
            prefetch_next_w_in(layer)
            cp("dve", qnP_s.ap.rearrange("p (g kh d) -> p kh g d", g=8, kh=2),
               qn_s.ap.rearrange("p (kh g d) -> p kh g d", kh=2, g=8), [qn_s], [qnP_s])
            bk = nb()
            for g_ in range(8):
                tr(bf(bk)[:, g_ * NS:(g_ + 1) * NS], qnP_s.ap[:, g_ * 128:(g_ + 1) * 128], ident.ap[0:NS, 0:NS],
                   [qnP_s, ident], [bk])
            cp("act", qTp.ap, bf(bk)[:, 0:8 * NS].rearrange("p (a t) -> p a t", a=8), [bk], [qTp])
            memset("pool", qbd, 0.0)
            cp("dve", qbd.ap[0:64, :, 0:8], qTp.ap[0:64, :, :].rearrange("p g b -> p b g"), [qTp], [qbd])
            cp("dve", qbd.ap[64:128, :, 8:16], qTp.ap[64:128, :, :].rearrange("p g b -> p b g"), [qTp], [qbd])
            S.dma("sp", ks[li][:, 0:127, :], ckd[li][:, 1:128, :], [], [])
            S.dma("sp", vs[li][:, 0:127, :], cvd[li][:, 1:128, :], [], [])
            S.dma("pool", ks[li][:, 127, :], kn_s.ap, [kn_s], [])
            S.dma("pool", vs[li][:, 127, :], vv_s.ap, [vv_s], [])
            tt("dve", prod.ap[0:NS, :].rearrange("p (a g d) -> p a g d", a=2, g=8),
               qn_s.ap.rearrange("p (a g d) -> p a g d", a=2, g=8),
               kn_s.ap.rearrange("p (a d) -> p a d", a=2).unsqueeze(2).to_broadcast([NS, 2, 8, 64]), ALU.mult,
               [qn_s, kn_s], [prod])
            S.op("dve", lambda e: e.tensor_reduce(out=snew.ap, in_=prod.ap[0:NS, :].rearrange("p (h d) -> p h d", d=64),
                                                  axis=AX.X, op=ALU.add), [prod], [snew])
            act(pnew.ap, snew.ap, AF.Exp, [snew], [pnew], scale=SHD ** -0.5)
            memset("pool", oacc, 0.0)
            memset("pool", dacc, 0.0)
            def q_stage1(b):
                kct, vct, ptb = kc_t[b % 2], vc_t[b % 2], PTb[b % 2]
                kb_, kt_ = kcb[b % 2], KTb[b % 2]
                S.dma("sp", kct.ap, ckd[li, b], [], [kct], grp=f"kc{b % 2}")
                S.dma("sp", vct.ap, cvd[li, b], [], [vct], grp=f"vc{b % 2}")
                cp("act", kb_.ap, kct.ap, [kct], [kb_])
                bk = nb()
                tr(bf(bk)[:, 0:128], kb_.ap, ident.ap, [kb_, ident], [bk])
                cp("dve", kt_.ap, bf(bk)[:, 0:128], [bk], [kt_])
                bk = nb()
                mm(bk.ap[:, 0:SH], kt_.ap, qbd.ap[:, b, :], True, True, [kt_, qbd], [bk])
                stt("dve", STt.ap, bk.ap[:, 0:SH], SHD ** -0.5, cs("alibi"), ALU.mult, ALU.add, [bk, CST], [STt])
                act(ptb.ap, STt.ap, AF.Exp, [STt], [ptb])

            def q_stage2(b):
                vct, ptb, rb_ = vc_t[b % 2], PTb[b % 2], rhsb[b % 2]
                tt("dve", rb_.ap.rearrange("p (a g d) -> p a g d", a=2, g=8),
                   vct.ap.rearrange("p (a d) -> p a d", a=2).unsqueeze(2).to_broadcast([128, 2, 8, 64]),
                   ptb.ap.rearrange("p (a g) -> p a g", a=2).unsqueeze(3).to_broadcast([128, 2, 8, 64]), ALU.mult,
                   [vct, ptb], [rb_])
                for n in range(2):
                    bk = nb()
                    mm(bk.ap[0:NS, :], onehotB.ap[:, b, :], rb_.ap[:, n * 512:(n + 1) * 512], True, True,
                       [onehotB, rb_], [bk])
                    tt("dve", oacc.ap[:, n * 512:(n + 1) * 512], bk.ap[0:NS, :], oacc.ap[:, n * 512:(n + 1) * 512],
                       ALU.add, [bk, oacc], [oacc])
                bk = nb()
                mm(bk.ap[0:NS, 0:SH], onehotB.ap[:, b, :], ptb.ap, True, True, [onehotB, ptb], [bk])
                tt("dve", dacc.ap, bk.ap[0:NS, 0:SH], dacc.ap, ALU.add, [bk, dacc], [dacc])

            q_stage1(0)
            for b in range(NS):
                if b + 1 < NS:
                    q_stage1(b + 1)
                q_stage2(b)
            tt("dve", prod.ap[0:NS, :].rearrange("p (a g d) -> p a g d", a=2, g=8),
               vv_s.ap.rearrange("p (a d) -> p a d", a=2).unsqueeze(2).to_broadcast([NS, 2, 8, 64]),
               pnew.ap.rearrange("p (a g) -> p a g", a=2).unsqueeze(3).to_broadcast([NS, 2, 8, 64]), ALU.mult,
               [vv_s, pnew], [prod])
            tt("dve", oacc.ap, oacc.ap, prod.ap[0:NS, :], ALU.add, [oacc, prod], [oacc])
            tt("dve", dacc.ap, dacc.ap, pnew.ap, ALU.add, [dacc, pnew], [dacc])
            tt("dve", dacc.ap, dacc.ap, esink.ap[0:NS, :], ALU.add, [dacc, esink], [dacc])
            S.op("dve", lambda e: e.reciprocal(out=dacc.ap, in_=dacc.ap), [dacc], [dacc])
            tt("dve", oacc.ap.rearrange("p (h d) -> p h d", d=64), oacc.ap.rearrange("p (h d) -> p h d", d=64),
               dacc.ap.unsqueeze(2).to_broadcast([NS, SH, 64]), ALU.mult, [oacc, dacc], [oacc])
            tt("dve", og_s.ap, oacc.ap, gs_s.ap, ALU.mult, [oacc, gs_s], [og_s])
            out_proj(og_s, ogT_s, 8, NS, xst)
            S.dma("pool", ys if layer == DEPTH - 1 else xsd, xst.ap, [xst], [])
    except _Stop:
        pass

    S.barrier()

    sems = {}
    for e in ("pe", "act", "dve", "pool"):
        sems[e] = es.enter_context(nc.semaphore(f"s_{e}"))
    for i, r in enumerate(S.dres):
        r.dsem = es.enter_context(nc.semaphore(f"d{i}_{r.name}"))

    def semof(key):
        return key.dsem if isinstance(key, DSem) else sems[key]

    def replay(name, e):
        for item in S.prog[name]:
            if item[0] == "wait":
                e.wait_ge(semof(item[1]), item[2])
            elif item[0] == "op":
                item[1](e).then_inc(sems[name], 1)
            else:
                _, o_, i_, r_, kw = item
                e.dma_start(out=o_, in_=i_, **kw).then_inc(r_.dsem, 16)

    with nc.Block() as block:
        @block.tensor
        def _(e):
            replay("pe", e)

        @block.scalar
        def _(e):
            replay("act", e)

        @block.vector
        def _(e):
            replay("dve", e)

        @block.gpsimd
        def _(e):
            replay("pool", e)

        @block.sync
        def _(e):
            replay("sp", e)
    es.close()
    return nc


_NC_CACHE = {}


def _get_nc(NCH, DEPTH):
    key = (NCH, DEPTH)
    if key not in _NC_CACHE:
        _NC_CACHE[key] = build_nc(NCH, DEPTH)
    return _NC_CACHE[key]


def kernel(x_prompt, x_sample, state_ret, cache_swa_k, cache_swa_v,
           norm_ret, w_in_ret, ret_out_norm, w_out_ret,
           norm_swa, w_in_swa, q_norm, k_norm, sinks, w_out_swa, _depth=4):
    f = lambda a: np.ascontiguousarray(np.asarray(a, dtype=np.float32))
    x_prompt, x_sample, state_ret = f(x_prompt), f(x_sample), f(state_ret)
    cache_swa_k, cache_swa_v = f(cache_swa_k), f(cache_swa_v)
    B, L, _ = x_prompt.shape
    NCH = L // 128
    DEPTH = _depth
    nc = _get_nc(NCH, DEPTH)
    c, off, CW = _cst_layout()
    cst = np.concatenate([c[k] for k in CORDER], axis=1).astype(np.float32)
    n_cores = 8
    nb_ = x_sample.shape[0] // n_cores
    assert nb_ == NS
    shared = {
        "norm_ret": f(norm_ret), "w_in_ret": f(w_in_ret), "ret_out_norm": f(ret_out_norm),
        "w_out_ret": f(w_out_ret), "norm_swa": f(norm_swa), "w_in_swa": f(w_in_swa),
        "q_norm": f(q_norm), "k_norm": f(k_norm), "sinks": f(sinks).reshape(sinks.shape[0], SH),
        "w_out_swa": f(w_out_swa), "cst": cst,
    }
    NSWA = cache_swa_k.shape[0]
    in_maps = []
    for core in range(n_cores):
        bsl = slice(core * NS, (core + 1) * NS)
        m = dict(shared)
        m["xp"] = f(x_prompt[(core // 2) % B])
        m["xs"] = f(x_sample[bsl, 0, :])
        m["st"] = f(state_ret[:, bsl])
        m["ck"] = f(cache_swa_k[:, bsl].reshape(NSWA, NS, 128, 128))
        m["cv"] = f(cache_swa_v[:, bsl].reshape(NSWA, NS, 128, 128))
        in_maps.append(m)
    res = run_bass_kernel_spmd(nc, in_maps, core_ids=list(range(n_cores)))
    R = res.results
    NRET = (DEPTH + 1) // 2
    nswa = DEPTH // 2
    y_prompt = np.stack([R[2 * b]["yp"] for b in range(B)], 0)
    y_sample = np.concatenate([R[cidx]["ys"] for cidx in range(n_cores)], 0)[:, None, :]
    new_ret_prompt = np.stack([R[2 * b]["rp"][:NRET] for b in range(B)], 1)
    new_ret_sample = np.concatenate([R[cidx]["rs"][:NRET] for cidx in range(n_cores)], 1)
    kp = np.stack([R[2 * b]["kp"][:nswa] for b in range(B)], 1).reshape(nswa, B, 128, SKV, SHD)
    vp = np.stack([R[2 * b]["vp"][:nswa] for b in range(B)], 1).reshape(nswa, B, 128, SKV, SHD)
    ksn = np.concatenate([R[cidx]["ks"][:nswa] for cidx in range(n_cores)], 1).reshape(nswa, -1, 128, SKV, SHD)
    vsn = np.concatenate([R[cidx]["vs"][:nswa] for cidx in range(n_cores)], 1).reshape(nswa, -1, 128, SKV, SHD)
    return (y_prompt.astype(np.float32), y_sample.astype(np.float32), new_ret_prompt.astype(np.float32),
            new_ret_sample.astype(np.float32), kp.astype(np.float32), vp.astype(np.float32),
            ksn.astype(np.float32), vsn.astype(np.float32))
```
